# Optimizing a Trainium2 kernel written in Bass

```python
import math
import jax, jax.numpy as jnp
from jax import lax
import numpy as np

D_MODEL = 1024
BATCH = 1
SEQ = 16384
DEPTH = 2

HEAD_DIM = 64
N_ATTN_HEADS = 12
ATTN_WIDTH = N_ATTN_HEADS * HEAD_DIM
N_CONV_GROUPS = 4
CONV_WIDTH = D_MODEL - ATTN_WIDTH
CONV_K = 3
IN_PROJ_WIDTH = 3 * ATTN_WIDTH + 3 * CONV_WIDTH
DILATED_PATTERNS = ((128, 1), (512, 4), (2048, 16))
MAX_SPAN = max(w for w, _ in DILATED_PATTERNS)
Q_BLOCK = 128
ROPE_THETA = 10000.0
EPS = 1e-6
PEER_HEADS = 8
PEER_NKEYS = 128
PEER_N_EXPERTS = PEER_NKEYS * PEER_NKEYS
PEER_DKEY = 256
PEER_TOPK = 16
TOKEN_BLOCK = 128
N_MOD = 6

kernel_name = "hybrid_dilated_attn_shortconv_peer"


def _rms(x, g):
    xf = x.astype(jnp.float32)
    y = xf * lax.rsqrt(jnp.mean(xf * xf, axis=-1, keepdims=True) + EPS)
    return (y * g.astype(jnp.float32)).astype(x.dtype)


def _rope(x, positions):
    half = HEAD_DIM // 2
    freq = ROPE_THETA ** (-jnp.arange(half, dtype=jnp.float32) / half)
    ang = positions.astype(jnp.float32)[..., None] * freq
    cos = jnp.cos(ang)[:, :, None, :]
    sin = jnp.sin(ang)[:, :, None, :]
    xf = x.astype(jnp.float32)
    x1, x2 = xf[..., :half], xf[..., half:]
    return jnp.concatenate([x1 * cos - x2 * sin, x2 * cos + x1 * sin], axis=-1).astype(x.dtype)


def _dilated_attention(q, k, v):
    B, S, H, D = q.shape
    scale = D ** -0.5
    n_blk = S // Q_BLOCK
    dists = np.concatenate([np.arange(w // d + 1) * d for w, d in DILATED_PATTERNS]).astype(np.int32)
    rel = (np.arange(Q_BLOCK, dtype=np.int32)[:, None] - dists[None, :])
    local = jnp.asarray(rel + MAX_SPAN)
    rel_j = jnp.asarray(rel)
    kp = jnp.pad(k, ((0, 0), (MAX_SPAN, 0), (0, 0), (0, 0)))
    vp = jnp.pad(v, ((0, 0), (MAX_SPAN, 0), (0, 0), (0, 0)))
    qb = q.reshape(B, n_blk, Q_BLOCK, H, D).transpose(1, 0, 2, 3, 4)

    def block(args):
        i, qblk = args
        t0 = i * Q_BLOCK
        kw = lax.dynamic_slice_in_dim(kp, t0, Q_BLOCK + MAX_SPAN, axis=1)
        vw = lax.dynamic_slice_in_dim(vp, t0, Q_BLOCK + MAX_SPAN, axis=1)
        kg = jnp.take(kw, local, axis=1)
        vg = jnp.take(vw, local, axis=1)
        logits = jnp.einsum('bqhd,bqkhd->bhqk', qblk.astype(jnp.float32), kg.astype(jnp.float32)) * scale
        valid = (t0 + rel_j) >= 0
        logits = jnp.where(valid[None, None], logits, -jnp.inf)
        p = jax.nn.softmax(logits, axis=-1)
        out = jnp.einsum('bhqk,bqkhd->bqhd', p, vg.astype(jnp.float32))
        return out.astype(q.dtype)

    out = lax.map(block, (jnp.arange(n_blk, dtype=jnp.int32), qb))
    return out.transpose(1, 0, 2, 3, 4).reshape(B, S, H, D)


def _short_conv(bg, cg, xv, w_conv):
    S = xv.shape[1]
    u = cg * xv
    up = jnp.pad(u, ((0, 0), (CONV_K - 1, 0), (0, 0)))
    y = up[:, 0:S] * w_conv[0]
    for kk in range(1, CONV_K):
        y = y + up[:, kk:kk + S] * w_conv[kk]
    return bg * y


def _peer(h, w_q, sub_keys, u_tab, v_tab):
    B, S, D = h.shape
    q = (h @ w_q).reshape(B, S, PEER_HEADS, 2, PEER_DKEY // 2)
    scores = jnp.einsum('bshpc,hpnc->bshpn', q.astype(jnp.float32), sub_keys.astype(jnp.float32))
    s_top, i_top = lax.top_k(scores, PEER_TOPK)
    cand = s_top[..., 0, :, None] + s_top[..., 1, None, :]
    cand_idx = i_top[..., 0, :, None] * PEER_NKEYS + i_top[..., 1, None, :]
    cand = cand.reshape(B, S, PEER_HEADS, PEER_TOPK * PEER_TOPK)
    cand_idx = cand_idx.reshape(B, S, PEER_HEADS, PEER_TOPK * PEER_TOPK)
    best, pos = lax.top_k(cand, PEER_TOPK)
    experts = jnp.take_along_axis(cand_idx, pos, axis=-1)
    gates = jax.nn.softmax(best, axis=-1).astype(h.dtype)
    T = B * S
    n_blk = T // TOKEN_BLOCK
    hb = h.reshape(n_blk, TOKEN_BLOCK, D)
    eb = experts.reshape(n_blk, TOKEN_BLOCK, PEER_HEADS, PEER_TOPK)
    gb = gates.reshape(n_blk, TOKEN_BLOCK, PEER_HEADS, PEER_TOPK)

    def block(args):
        hx, e, g = args
        ue = jnp.take(u_tab, e, axis=0)
        act = jax.nn.gelu(jnp.einsum('td,thkd->thk', hx, ue), approximate=False)
        ve = jnp.take(v_tab, e, axis=0)
        return jnp.einsum('thk,thkd->td', g * act, ve)

    out = lax.map(block, (hb, eb, gb))
    return out.reshape(B, S, D)


def _layer(x, c, positions, w_ada, b_ada, norm_mix, norm_ffn, w_in, q_norm, k_norm,
           conv_w, w_out, peer_wq, peer_keys, peer_u, peer_v):
    B, S, _ = x.shape
    mod = (c @ w_ada + b_ada)[:, None, :]
    sh1, sc1, gt1, sh2, sc2, gt2 = jnp.split(mod, N_MOD, axis=-1)
    h = _rms(x, norm_mix) * (1 + sc1) + sh1
    proj = h @ w_in
    A, C = ATTN_WIDTH, CONV_WIDTH
    q, k, v, bg, cg, xv = jnp.split(proj, [A, 2 * A, 3 * A, 3 * A + C, 3 * A + 2 * C], axis=-1)
    q = _rope(_rms(q.reshape(B, S, N_ATTN_HEADS, HEAD_DIM), q_norm), positions)
    k = _rope(_rms(k.reshape(B, S, N_ATTN_HEADS, HEAD_DIM), k_norm), positions)
    v = v.reshape(B, S, N_ATTN_HEADS, HEAD_DIM)
    attn = _dilated_attention(q, k, v).reshape(B, S, ATTN_WIDTH)
    conv = _short_conv(bg, cg, xv, conv_w)
    mix = jnp.concatenate([attn, conv], axis=-1) @ w_out
    x = x + gt1 * mix
    h = _rms(x, norm_ffn) * (1 + sc2) + sh2
    x = x + gt2 * _peer(h, peer_wq, peer_keys, peer_u, peer_v)
    return x


def setup_inputs(seed: int = 0) -> dict:
    key = jax.random.key(seed)
    ks = jax.random.split(key, 16)
    f32 = jnp.float32
    D = D_MODEL
    x = jax.random.normal(ks[0], (BATCH, SEQ, D), f32)
    c = jax.random.normal(ks[1], (BATCH, D), f32)
    positions = jnp.broadcast_to(jnp.arange(SEQ, dtype=jnp.int32)[None, :], (BATCH, SEQ))
    w_ada = jax.random.normal(ks[2], (DEPTH, D, N_MOD * D), f32) * (0.1 * D ** -0.5)
    b_ada = jax.random.normal(ks[3], (DEPTH, N_MOD * D), f32) * 0.02
    norm_mix = 1.0 + 0.02 * jax.random.normal(ks[4], (DEPTH, D), f32)
    norm_ffn = 1.0 + 0.02 * jax.random.normal(ks[5], (DEPTH, D), f32)
    w_in = jax.random.normal(ks[6], (DEPTH, D, IN_PROJ_WIDTH), f32) * D ** -0.5
    q_norm = 1.0 + 0.02 * jax.random.normal(ks[7], (DEPTH, HEAD_DIM), f32)
    k_norm = 1.0 + 0.02 * jax.random.normal(ks[8], (DEPTH, HEAD_DIM), f32)
    conv_w = jax.random.normal(ks[9], (DEPTH, CONV_K, CONV_WIDTH), f32) * CONV_K ** -0.5
    w_out = jax.random.normal(ks[10], (DEPTH, D, D), f32) * D ** -0.5
    peer_wq = jax.random.normal(ks[11], (DEPTH, D, PEER_HEADS * PEER_DKEY), f32) * D ** -0.5
    peer_keys = jax.random.normal(ks[12], (DEPTH, PEER_HEADS, 2, PEER_NKEYS, PEER_DKEY // 2), f32) * (PEER_DKEY // 2) ** -0.5
    peer_u = jax.random.normal(ks[13], (DEPTH, PEER_N_EXPERTS, D), f32) * D ** -0.5
    peer_v = jax.random.normal(ks[14], (DEPTH, PEER_N_EXPERTS, D), f32) * PEER_HEADS ** -0.5
    return {"x": x, "c": c, "positions": positions, "w_ada": w_ada, "b_ada": b_ada,
            "norm_mix": norm_mix, "norm_ffn": norm_ffn, "w_in": w_in, "q_norm": q_norm,
            "k_norm": k_norm, "conv_w": conv_w, "w_out": w_out, "peer_wq": peer_wq,
            "peer_keys": peer_keys, "peer_u": peer_u, "peer_v": peer_v}


def reference(x, c, positions, w_ada, b_ada, norm_mix, norm_ffn, w_in, q_norm, k_norm,
              conv_w, w_out, peer_wq, peer_keys, peer_u, peer_v):
    for l in range(DEPTH):
        x = _layer(x, c, positions, w_ada[l], b_ada[l], norm_mix[l], norm_ffn[l], w_in[l],
                   q_norm[l], k_norm[l], conv_w[l], w_out[l], peer_wq[l], peer_keys[l],
                   peer_u[l], peer_v[l])
    return x
```

```python
import math
from contextlib import ExitStack
import numpy as np
import ml_dtypes
import concourse.bass as bass
import concourse.mybir as mybir
from concourse.bass_utils import run_bass_kernel_spmd

F32 = mybir.dt.float32
BF16 = mybir.dt.bfloat16
I32 = mybir.dt.int32
U32 = mybir.dt.uint32
AF = mybir.ActivationFunctionType
ALU = mybir.AluOpType
AX = mybir.AxisListType

NCORES = 8
SEQ = 16384
D = 1024
TPC = SEQ // NCORES
NT = TPC // 128
EPS = 1e-6
NEXP = 16384


class Sched:
    COMPUTE = ("act", "dve", "pool", "pe")
    NDMA = 24

    def __init__(self, nc, stack):
        self.nc = nc
        self.ops = {e: [] for e in ("sync",) + self.COMPUTE}
        self.sem = {e: stack.enter_context(nc.semaphore("s_" + e)) for e in self.COMPUTE}
        self.cnt = {e: 0 for e in self.COMPUTE}
        self.dsem = [stack.enter_context(nc.semaphore("d%d" % i)) for i in range(self.NDMA)]
        self.dcnt = [0] * self.NDMA
        self.drr = 0
        self.waited = {e: {} for e in self.ops}
        self.last_w = {}
        self.readers = {}
        self.all_events = {}
        self.rec = None
        self.csem = [stack.enter_context(nc.semaphore("cv%d" % i)) for i in range(8)]
        self.ccnt = [0] * 8
        self.crr = 0
        self.cvt_events = {}

    def _collect(self, eng, reads, writes):
        evs = []
        for k in reads:
            if k in self.last_w:
                evs.append(self.last_w[k])
        for k in writes:
            if k in self.last_w:
                evs.append(self.last_w[k])
            evs.extend(self.readers.get(k, ()))
        out = []
        for (sid, sem, val, e) in evs:
            if eng == "pe" and e == "pe":
                continue
            if self.waited[eng].get(sid, 0) >= val:
                continue
            self.waited[eng][sid] = val
            out.append((sem, val))
        return out

    def _record(self, ev, reads, writes):
        for k in reads:
            self.readers.setdefault(k, []).append(ev)
        for k in writes:
            self.last_w[k] = ev
            self.readers[k] = []
        self.all_events[ev[0]] = ev

    def op(self, eng, fn, reads=(), writes=()):
        if self.rec is not None:
            self.rec.append(("op", eng, fn, tuple(reads), tuple(writes)))
            return
        waits = self._collect(eng, reads, writes)
        self.cnt[eng] += 1
        ev = ("c_" + eng, self.sem[eng], self.cnt[eng], eng)
        self.ops[eng].append((waits, fn, (self.sem[eng], 1)))
        self._record(ev, reads, writes)

    def dma(self, queue, fn, reads=(), writes=()):
        if self.rec is not None:
            self.rec.append(("dma", queue, fn, tuple(reads), tuple(writes)))
            return
        j = self.drr
        self.drr = (j + 1) % self.NDMA
        waits = self._collect(queue, reads, writes)
        sid = "d%d" % j
        if self.dcnt[j] > 0 and self.waited[queue].get(sid, 0) < self.dcnt[j]:
            waits.append((self.dsem[j], self.dcnt[j]))
            self.waited[queue][sid] = self.dcnt[j]
        self.dcnt[j] += 16
        ev = (sid, self.dsem[j], self.dcnt[j], "dma")
        self.ops[queue].append((waits, fn, (self.dsem[j], 16)))
        self._record(ev, reads, writes)

    def dma_cvt(self, queue, fn):
        j = self.crr
        self.crr = (j + 1) % 8
        waits = []
        sid = "cv%d" % j
        if self.ccnt[j] > 0 and self.waited[queue].get(sid, 0) < self.ccnt[j]:
            waits.append((self.csem[j], self.ccnt[j]))
            self.waited[queue][sid] = self.ccnt[j]
        self.ccnt[j] += 16
        self.cvt_events[sid] = (sid, self.csem[j], self.ccnt[j], "dma")
        self.ops[queue].append((waits, fn, (self.csem[j], 16)))

    def wait_cvt(self, queue):
        waits = []
        for (sid, sem, val, e) in self.cvt_events.values():
            if self.waited[queue].get(sid, 0) >= val:
                continue
            self.waited[queue][sid] = val
            waits.append((sem, val))
        if waits:
            self.ops[queue].append((waits, None, None))

    def replay_records(self, recs):
        for (kind, eng, fn, reads, writes) in recs:
            (self.op if kind == "op" else self.dma)(eng, fn, reads, writes)

    def barrier(self):
        evs = list(self.all_events.values())
        for eng in self.ops:
            waits = []
            for (sid, sem, val, e) in evs:
                if self.waited[eng].get(sid, 0) >= val:
                    continue
                self.waited[eng][sid] = val
                waits.append((sem, val))
            if waits:
                self.ops[eng].append((waits, None, None))

    def emit(self):
        nc = self.nc
        with nc.Block() as block:
            def replay(name):
                def run(eng):
                    for (waits, fn, inc) in self.ops[name]:
                        for (sem, val) in waits:
                            eng.wait_ge(sem, val)
                        if fn is not None:
                            ins = fn(eng)
                            ins.then_inc(inc[0], inc[1])
                return run
            block.sync(replay("sync"))
            block.scalar(replay("act"))
            block.vector(replay("dve"))
            block.gpsimd(replay("pool"))
            block.tensor(replay("pe"))


def _mask_table():
    kk = np.arange(128)[:, None]
    qq = np.arange(128)[None, :]
    blocks = []
    for delta in range(-3, 20):
        d = delta * 128 + qq - kk
        m = ((d >= 0) & (d <= 128)).astype(np.float32)
        m += ((d >= 0) & (d <= 512) & (d % 4 == 0)).astype(np.float32)
        m += ((d >= 0) & (d <= 2048) & (d % 16 == 0)).astype(np.float32)
        blocks.append(m)
    return np.concatenate(blocks, axis=1).astype(ml_dtypes.bfloat16)


def _consts():
    c = {}
    c["identb"] = np.eye(128, dtype=np.float32).astype(ml_dtypes.bfloat16)
    c["identf"] = np.eye(128, dtype=np.float32)
    c["mask"] = _mask_table()
    half = 32
    freq = (10000.0 ** (-np.arange(half, dtype=np.float32) / half)).astype(np.float32)
    c["freq"] = np.broadcast_to(freq[None, :], (128, half)).copy()
    sh = np.stack([np.eye(128, k=2), np.eye(128, k=1), np.eye(128, k=-126), np.eye(128, k=-127)])
    c["shm"] = np.ascontiguousarray(sh.transpose(1, 0, 2)).astype(np.float32)
    c["iota16"] = np.broadcast_to(np.arange(16, dtype=np.float32)[None, :], (128, 16)).copy()
    return c


class Builder:
    def __init__(self, plan, ext):
        self.plan = plan
        self.ext = ext
        self.nc = bass.Bass("TRN2", target_bir_lowering=False)
        self.st = ExitStack()
        self.S = None
        self.wkeys = [[] for _ in range(8)]

    def dram(self, name, shape, dt, kind=None):
        if kind is None:
            k = self.ext.get(name)
            kind = {"in": "ExternalInput", "out": "ExternalOutput", None: "Internal"}[k]
        return self.nc.dram_tensor(name, list(shape), dt, kind=kind).ap()

    def sb(self, name, shape, dt):
        return self.st.enter_context(self.nc.sbuf_tensor(name, list(shape), dt))

    def ps(self, name, shape, dt):
        return self.st.enter_context(self.nc.psum_tensor(name, list(shape), dt))

    def build(self):
        nc = self.nc
        with self.st:
            self.S = Sched(nc, self.st)
            self._declare()
            self._setup()
            for step in self.plan:
                getattr(self, "_step_" + step[0])(*step[1:])
            self._finish()
            self.S.emit()
        return nc

    IN_SPECS = {
        "x_m2": ("xm2", [TPC, D], F32), "x_m1": ("xm1", [TPC, D], F32), "x_0": ("x0", [TPC, D], F32), "cB": ("cB", [128, 8, 128], F32), "posi": ("posi", [128, 3 * NT], I32),
        "w_ada": ("w_ada", [2, D, 6144], F32), "brow": ("brow", [2, 128, 8192], F32),
        "w_in": ("w_in", [2, D, 3072], F32), "w_out": ("w_out", [2, D, D], F32), "w_q": ("w_q", [2, D, 2048], F32),
        "qkg": ("qkg", [2, 128, 128], F32), "convw": ("convw", [2, 128, 768], F32),
        "keysT": ("keysT", [2, 128, 2048], F32), "p_u": ("peer_u", [2, NEXP, D], F32),
        "p_v": ("peer_v", [2, NEXP, D], F32), "c_identb": ("identb", [128, 128], BF16),
        "c_identf": ("identf", [128, 128], F32), "c_mask": ("mask", [128, 23 * 128], BF16),
        "c_freq": ("freq", [128, 32], F32), "c_shm": ("shm", [128, 4, 128], F32),
        "c_iota16": ("iota16", [128, 16], F32), "c_flag": ("flag", [128, 2], F32),
    }

    def __getattr__(self, attr):
        specs = type(self).IN_SPECS
        if attr in specs:
            name, shape, dt = specs[attr]
            ap = self.dram(name, shape, dt, "ExternalInput")
            self.__dict__[attr] = ap
            self.__dict__.setdefault("used_inputs", []).append(name)
            return ap
        raise AttributeError(attr)

    def _declare(self):
        self.used_inputs = self.__dict__.get("used_inputs", [])
        self.xs = {"m1": self.dram("xs_m1", [TPC, D], F32, "Internal"), "c0": self.dram("xs_c0", [TPC, D], F32, "Internal")}
        self.qtd = self.dram("qtd", [6, 128, TPC], BF16, "Internal")
        self.uvb = self.dram("uvb", [2, NEXP, 2 * D], BF16, "Internal")
        self.x_out = self.dram("x_out", [TPC, D], F32) if self.ext.get("x_out") else None
        self.kv = {}

    def kvset(self, name):
        if name not in self.kv:
            self.kv[name] = dict(
                kt=self.dram(name + "_kt", [6, 128, TPC], BF16),
                ve=self.dram(name + "_ve", [6, NT, 128, 130], BF16),
                u=self.dram(name + "_u", [2, 256], F32),
            )
        return self.kv[name]

    def _setup(self):
        S = self.S
        sb, ps = self.sb, self.ps
        self.PB = [ps("pb%d" % i, [128, 512], F32) for i in range(6)]
        self.PT = [ps("pt%d" % i, [128, 1024], BF16) for i in range(2)]
        self.IDB = sb("idb", [128, 128], BF16)
        self.IDF = sb("idf", [128, 128], F32)
        self.MASK = sb("maskt", [128, 23 * 128], BF16)
        self.FREQ = sb("freqt", [128, 32], F32)
        self.SHM = sb("shmt", [128, 4, 128], F32)
        self.IOTA16 = sb("iota16t", [128, 16], F32)
        self.FLAG = sb("flagt", [128, 2], F32)
        self.POSI = sb("posit", [128, 3 * NT], I32)
        self.POSF = sb("posft", [128, 3 * NT], F32)
        self.COS = sb("cost", [128, NT, 32], F32)
        self.SIN = sb("sint", [128, NT, 32], F32)
        self.MODT = sb("modt", [128, 6144], F32)
        self.WBUF = sb("wbuf", [128, 8, 3072], BF16)
        self.MIX = sb("mixcat", [128, NT, 1024], BF16)
        self.PROJ = sb("proj", [128, 3072], F32)
        self.CB = self.PROJ[:, 0:1024].rearrange("p (a b) -> p a b", b=128)
        self.XT = [sb("xt%d" % i, [128, 1024], F32) for i in range(2)]
        self.HB = sb("hb", [128, 1024], BF16)
        self.HT = sb("ht", [128, 8, 128], BF16)
        self.JUNK = sb("junk", [128, 1536], F32)
        self.HF = self.JUNK[:, 0:1024]
        self.SMALL = sb("small", [128, 64], F32)
        self.QKG = sb("qkgt", [128, 128], F32)
        self.HBD = [sb("hbd%d" % i, [128, 1024], BF16) for i in range(2)]
        eg2 = sb("eg2", [128, 256], F32)
        self.EIDXB = [None, eg2[:, 0:128].bitcast(I32)]
        self.GATEB = [None, eg2[:, 128:256].rearrange("p (a b) -> p a b", b=16)]
        self.CONVW = sb("convwt", [128, 768], F32)
        self.TR = [sb("tr%d" % i, [128, 24, 32], F32) for i in range(3)]
        self.ANG = self.TR[0][:, 0:NT, :]
        self.ANG2 = self.TR[1][:, 0:NT, :]
        self.KR = self.TR[2][:, 0:NT, :]
        self.QR = sb("qr", [128, 1536], BF16)
        self.QTS = sb("qts", [128, 768], BF16)
        self.KTS = sb("kts", [128, 768], BF16)
        self.VES = sb("ves", [128, 12, 65], BF16)
        self.UC = sb("uc", [128, 256], F32)
        self.UW = [sb("uw%d" % i, [128, 768], F32) for i in range(2)]
        self.UW0 = sb("uwz", [128, 768], F32)
        self.B0 = sb("b0", [128, 256], F32)
        self.UH = sb("uh", [128, 256], F32)
        self.UWH = sb("uwh", [128, 768], F32)
        self.KT = sb("ktb", [128, 2 * TPC], BF16)
        self.QT = sb("qtb", [128, TPC], BF16)
        self.VE = sb("veb", [128, 2 * NT, 130], BF16)
        trb = [self.TR[i][:].rearrange("p a b -> p (a b)").bitcast(BF16) for i in range(3)]
        self.PM = [trb[0][:, 0:512], trb[0][:, 512:1024], trb[0][:, 1024:1536], trb[1][:, 0:512]]
        self.PE_ = [trb[1][:, 512:1024], trb[1][:, 1024:1536], trb[2][:, 0:512], trb[2][:, 512:1024]]
        self.SBK = [(self.PB[0][:], "PB0"), (self.PB[1][:], "PB1"),
                    (self.PT[0][:].bitcast(F32), "PT0"), (self.PT[1][:].bitcast(F32), "PT1")]
        self.RC = sb("rc", [128, 8], F32)
        self.SC = self.PROJ[:, 0:2048].rearrange("p (a b) -> p a b", b=128)
        self.QPT = self.PROJ[:, 2048:3072].bitcast(BF16).rearrange("p (a b) -> p a b", b=128)
        self.CAND = self.KT[:].bitcast(F32).rearrange("p (a b) -> p a b", b=256)
        self.OH = self.VE[:].rearrange("p a b -> p (a b)")[:, 0:4096].bitcast(F32).rearrange(
            "p (h a b) -> p h a b", a=16, b=16)
        self.QP = self.QT[:]
        self.KEYS = self.MASK[:, 0:2048]
        def carve(base, off, n, dt, pat=None, **kw):
            v = base[:, off:off + n]
            if dt is not F32:
                v = v.bitcast(dt)
            return v.rearrange(pat, **kw) if pat else v
        u0, u1, uz, uh_ = self.UW[0], self.UW[1], self.UW0, self.UWH
        self.SCM = carve(u0, 0, 256, F32)
        self.V16 = carve(u0, 256, 256, F32, "p (a b) -> p a b", b=16)
        self.IX = carve(u0, 512, 256, U32, "p (a b) -> p a b", b=16)
        self.IXF = carve(u1, 0, 256, F32, "p (a b) -> p a b", b=16)
        self.BEST = carve(u1, 256, 128, F32, "p (a b) -> p a b", b=16)
        self.POS = carve(u1, 384, 128, U32, "p (a b) -> p a b", b=16)
        self.R0 = carve(u1, 512, 128, U32, "p (a b) -> p a b", b=16)
        self.R1 = carve(u1, 640, 128, U32, "p (a b) -> p a b", b=16)
        self.R0F = carve(uz, 0, 128, F32, "p (a b) -> p a b", b=16)
        self.R1F = carve(uz, 128, 128, F32, "p (a b) -> p a b", b=16)
        self.ISEL = carve(uz, 256, 128, F32, "p (a b) -> p a b", b=16)
        self.JSEL = carve(uz, 384, 128, F32, "p (a b) -> p a b", b=16)
        self.EIDX = carve(uz, 512, 128, I32)
        self.GATE = carve(uz, 640, 128, F32, "p (a b) -> p a b", b=16)
        self.EIDXB[0] = self.EIDX
        self.GATEB[0] = self.GATE
        self.ACTV = carve(uh_, 0, 128, F32)
        self.ZZ = carve(uh_, 128, 128, F32)
        self.DK = [carve(uh_, 256 + 64 * i, 64, BF16) for i in range(4)]
        mixraw = self.MIX[:].rearrange("p a b -> p (a b)")
        self.WA = [mixraw[:, i * 6144:(i + 1) * 6144].bitcast(F32) for i in range(2)]
        self.UVG = [mixraw[:, i * 2048:(i + 1) * 2048] for i in range(8)]

        sy = lambda out, in_, r, w: S.dma("sync", lambda e: e.dma_start(out=out, in_=in_), r, w)
        sy(self.IDB[:], self.c_identb, (), ("IDB",))
        sy(self.IDF[:], self.c_identf, (), ("IDF",))
        sy(self.FREQ[:], self.c_freq, (), ("FREQ",))
        sy(self.SHM[:], self.c_shm, (), ("SHM",))
        sy(self.IOTA16[:], self.c_iota16, (), ("IOTA16",))
        sy(self.FLAG[:], self.c_flag, (), ("FLAG",))
        sy(self.POSI[:], self.posi, (), ("POSI",))
        S.dma("sync", lambda e: e.dma_start(out=self.xs["m1"], in_=self.x_m1), (), ("xs_m1",))
        S.dma("sync", lambda e: e.dma_start(out=self.xs["c0"], in_=self.x_0), (), ("xs_c0",))
        S.op("pool", lambda e: e.memset(self.VES[:], 1.0), (), ("VES",))
        S.op("dve", lambda e: e.tensor_copy(out=self.POSF[:], in_=self.POSI[:]), ("POSI",), ("POSF",))

    def _step_CVT(self, l):
        S = self.S
        for t, tab in enumerate((self.p_u, self.p_v)):
            for r0 in range(0, NEXP, 1024):
                S.dma_cvt("pool", lambda e, t=t, tab=tab, r0=r0: e.dma_start(
                    out=self.uvb[l, r0:r0 + 1024, t * D:(t + 1) * D], in_=tab[l, r0:r0 + 1024, :]))

    def xap(self, name):
        return self.x_m2 if name == "xm2" else self.xs[name]

    def _rope_tables(self, pc):
        S = self.S
        for tt in range(NT):
            S.op("dve", lambda e, tt=tt: e.tensor_scalar(
                out=self.ANG[:, tt, :], in0=self.FREQ[:], scalar1=self.POSF[:, pc * NT + tt:pc * NT + tt + 1],
                scalar2=None, op0=ALU.mult), ("FREQ", "POSF"), ("ANG",))
        S.op("dve", lambda e: e.tensor_scalar(out=self.ANG2, in0=self.ANG, scalar1=math.pi / 2, scalar2=None,
                                              op0=ALU.add), ("ANG",), ("ANG2",))
        MAGIC = 12582912.0
        C1 = 6.28125
        C2 = 2 * math.pi - 6.28125
        for (src, dst, key) in ((self.ANG, self.SIN, "SIN"), (self.ANG2, self.COS, "COS")):
            sk = "ANG" if src is self.ANG else "ANG2"
            S.op("dve", lambda e, src=src: e.tensor_scalar(out=self.KR, in0=src, scalar1=1.0 / (2 * math.pi),
                                                           scalar2=MAGIC, op0=ALU.mult, op1=ALU.add), (sk,), ("KR",))
            S.op("dve", lambda e: e.tensor_scalar(out=self.KR, in0=self.KR, scalar1=MAGIC, scalar2=None,
                                                  op0=ALU.subtract), ("KR",), ("KR",))
            S.op("dve", lambda e, src=src: e.scalar_tensor_tensor(out=src, in0=self.KR, scalar=-C1, in1=src,
                                                                  op0=ALU.mult, op1=ALU.add), ("KR", sk), (sk,))
            S.op("dve", lambda e, src=src: e.scalar_tensor_tensor(out=src, in0=self.KR, scalar=-C2, in1=src,
                                                                  op0=ALU.mult, op1=ALU.add), ("KR", sk), (sk,))
            S.op("act", lambda e, src=src, dst=dst: e.activation(out=dst[:], in_=src, func=AF.Sin), (sk,), (key,))
        S.op("dve", lambda e: e.tensor_copy(out=self.SMALL[:, 60:61], in_=self.SMALL[:, 60:61]), ("ANG", "ANG2", "KR"),
             ("T0", "T1", "T2"))

    def _rmsnorm(self, xt, xkey, a_off, sh_off, want_f32):
        S = self.S
        SM = self.SMALL
        S.op("act", lambda e: e.activation(out=self.HB[:], in_=xt[:], func=AF.Square,
                                           accum_out=SM[:, 0:1]), (xkey,), ("HB", "SM0"))
        S.op("act", lambda e: e.activation(out=SM[:, 1:2], in_=SM[:, 0:1], func=AF.Sqrt, bias=EPS, scale=1.0 / D),
             ("SM0",), ("SM1",))
        S.op("dve", lambda e: e.reciprocal(out=SM[:, 2:3], in_=SM[:, 1:2]), ("SM1",), ("SM2",))
        S.op("dve", lambda e: e.scalar_tensor_tensor(out=self.HF, in0=xt[:], scalar=SM[:, 2:3],
                                                     in1=self.MODT[:, a_off:a_off + 1024], op0=ALU.mult, op1=ALU.mult),
             (xkey, "SM2", "MODT"), ("JUNK",))
        if want_f32:
            S.op("pool", lambda e: e.tensor_tensor(out=self.HF, in0=self.HF, in1=self.MODT[:, sh_off:sh_off + 1024],
                                                   op=ALU.add), ("JUNK", "MODT"), ("JUNK",))
            S.op("act", lambda e: e.copy(out=self.HB[:], in_=self.HF), ("JUNK",), ("HB",))
        else:
            S.op("pool", lambda e: e.tensor_tensor(out=self.HB[:], in0=self.HF, in1=self.MODT[:, sh_off:sh_off + 1024],
                                                   op=ALU.add), ("JUNK", "MODT"), ("HB",))

    def _transpose8(self, src_fn, skey, dst, dkey, pt=0):
        S = self.S
        P = self.PT[pt]
        pk = "PT%d" % pt
        for c in range(8):
            S.op("pe", lambda e, c=c: e.transpose(out=P[:, c * 128:(c + 1) * 128], in_=src_fn(c), identity=self.IDB[:]),
                 (skey, "IDB"), (pk,))
        S.op("act", lambda e: e.copy(out=dst, in_=P[:, 0:1024]), (pk,), (dkey,))

    def _load_w(self, w_ap, l, col0, ncols, dst_col0, key):
        S = self.S
        step = 1024 if ncols % 1024 == 0 else 1536
        for c in range(8):
            for j in range(0, ncols, step):
                kk = "%s_%d_%d" % (key, c, dst_col0 + j)
                if kk not in self.wkeys[c]:
                    self.wkeys[c].append(kk)
                S.dma("pool", lambda e, c=c, j=j: e.dma_start(
                    out=self.WBUF[:, c, dst_col0 + j:dst_col0 + j + step],
                    in_=w_ap[l, c * 128:(c + 1) * 128, col0 + j:col0 + j + step]), (), (kk,))

    def _step_MOD(self, l):
        S = self.S
        S.barrier()
        S.dma("sync", lambda e: e.dma_start(out=self.CB, in_=self.cB), (), ("CB",))
        for half in range(2):
            for c in range(8):
                wa = self.WA[c % 2]
                wk = "WA%d" % (c % 2)
                S.dma("sync", lambda e, wa=wa, c=c, half=half: e.dma_start(
                    out=wa, in_=self.w_ada[l, c * 128:(c + 1) * 128, half * 3072:(half + 1) * 3072]), (), (wk,))
                for n in range(6):
                    S.op("pe", lambda e, wa=wa, c=c, n=n: e.matmul(
                        self.PB[n][:], lhsT=self.CB[:, c, :], rhs=wa[:, n * 512:(n + 1) * 512],
                        start=(c == 0), stop=(c == 7)), (wk, "CB"), ("PB%d" % n,))
            for n in range(6):
                col = half * 3072 + n * 512
                S.dma("sync", lambda e, col=col: e.dma_start(out=self.JUNK[:, 0:512], in_=self.brow[l, :, col:col + 512]),
                      (), ("JUNK",))
                S.op("dve", lambda e, n=n, col=col: e.tensor_tensor(
                    out=self.MODT[:, col:col + 512], in0=self.PB[n][:], in1=self.JUNK[:, 0:512], op=ALU.add),
                     ("PB%d" % n, "JUNK"), ("MODT",))
        for (slot, off) in ((1, 6144), (4, 7168)):
            S.dma("sync", lambda e, off=off: e.dma_start(out=self.JUNK[:, 0:1024], in_=self.brow[l, :, off:off + 1024]),
                  (), ("JUNK",))
            S.op("dve", lambda e, slot=slot: e.scalar_tensor_tensor(
                out=self.MODT[:, slot * 1024:(slot + 1) * 1024], in0=self.MODT[:, slot * 1024:(slot + 1) * 1024],
                scalar=1.0, in1=self.JUNK[:, 0:1024], op0=ALU.add, op1=ALU.mult), ("MODT", "JUNK"), ("MODT",))
        S.barrier()

    def _step_A(self, l, own, xname, pc, kv_only=False):
        S = self.S
        kv = self.kvset(own)
        xsrc = self.xap(xname)
        xkey = "xs_" + xname
        S.barrier()
        self._rope_tables(pc)
        self._load_w(self.w_in, l, 0, 3072, 0, "WBUF")
        S.dma("sync", lambda e: e.dma_start(out=self.QKG[:], in_=self.qkg[l]), (), ("QKG",))
        S.dma("sync", lambda e: e.dma_start(out=self.CONVW[:], in_=self.convw[l]), (), ("CONVW",))
        SM = self.SMALL
        for tt in range(NT):
            xt = self.XT[tt % 2]
            xk = "XT%d" % (tt % 2)
            S.dma("sync", lambda e, xt=xt, tt=tt: e.dma_start(out=xt[:], in_=xsrc[tt * 128:(tt + 1) * 128, :]),
                  (xkey,), (xk,))
            self._rmsnorm(xt, xk, 1024, 0, False)
            self._transpose8(lambda c: self.HB[:, c * 128:(c + 1) * 128], "HB",
                             self.HT[:].rearrange("p a b -> p (a b)"), "HT", 0)
            for n in range(6):
                for c in range(8):
                    S.op("pe", lambda e, n=n, c=c: e.matmul(
                        self.PB[n][:], lhsT=self.HT[:, c, :], rhs=self.WBUF[:, c, n * 512:(n + 1) * 512],
                        start=(c == 0), stop=(c == 7)), ("HT",) + tuple(self.wkeys[c]), ("PB%d" % n,))
                S.op("act", lambda e, n=n: e.copy(out=self.PROJ[:, n * 512:(n + 1) * 512], in_=self.PB[n][:]),
                     ("PB%d" % n,), ("PROJ",))
            S.op("dve", lambda e: e.tensor_tensor(out=self.JUNK[:], in0=self.PROJ[:, 0:1536], in1=self.PROJ[:, 0:1536],
                                                  op=ALU.mult), ("PROJ",), ("JUNK",))
            S.op("dve", lambda e: e.tensor_reduce(out=SM[:, 8:32], in_=self.JUNK[:].rearrange("p (h d) -> p h d", d=64),
                                                  axis=AX.X, op=ALU.add), ("JUNK",), ("SM8",))
            S.op("act", lambda e: e.activation(out=SM[:, 32:56], in_=SM[:, 8:32], func=AF.Sqrt, bias=EPS, scale=1.0 / 64),
                 ("SM8",), ("SM32",))
            S.op("dve", lambda e: e.reciprocal(out=SM[:, 8:32], in_=SM[:, 32:56]), ("SM32",), ("SM8",))
            S.op("dve", lambda e: e.tensor_tensor(
                out=self.JUNK[:].rearrange("p (h d) -> p h d", d=64),
                in0=self.PROJ[:, 0:1536].rearrange("p (h d) -> p h d", d=64),
                in1=SM[:, 8:32].unsqueeze(2).to_broadcast([128, 24, 64]), op=ALU.mult), ("PROJ", "SM8"), ("JUNK",))
            for s_ in range(2):
                S.op("pool", lambda e, s_=s_: e.tensor_tensor(
                    out=self.JUNK[:, s_ * 768:(s_ + 1) * 768].rearrange("p (h d) -> p h d", d=64),
                    in0=self.JUNK[:, s_ * 768:(s_ + 1) * 768].rearrange("p (h d) -> p h d", d=64),
                    in1=self.QKG[:, s_ * 64:(s_ + 1) * 64].unsqueeze(1).to_broadcast([128, 12, 64]), op=ALU.mult),
                     ("JUNK", "QKG"), ("JUNK",))
            Q3 = self.JUNK[:].rearrange("p (h d) -> p h d", d=64)
            x1, x2 = Q3[:, :, 0:32], Q3[:, :, 32:64]
            cosb = self.COS[:, tt, :].unsqueeze(1).to_broadcast([128, 24, 32])
            sinb = self.SIN[:, tt, :].unsqueeze(1).to_broadcast([128, 24, 32])
            QR3 = self.QR[:].rearrange("p (h d) -> p h d", d=64)
            T = self.TR
            S.op("dve", lambda e, cosb=cosb: e.tensor_tensor(out=T[0][:], in0=x1, in1=cosb, op=ALU.mult), ("JUNK", "COS"), ("T0",))
            S.op("pool", lambda e, sinb=sinb: e.tensor_tensor(out=T[1][:], in0=x2, in1=sinb, op=ALU.mult), ("JUNK", "SIN"), ("T1",))
            S.op("dve", lambda e: e.tensor_tensor(out=QR3[:, :, 0:32], in0=T[0][:], in1=T[1][:], op=ALU.subtract),
                 ("T0", "T1"), ("QR",))
            S.op("dve", lambda e, cosb=cosb: e.tensor_tensor(out=T[0][:], in0=x2, in1=cosb, op=ALU.mult), ("JUNK", "COS"), ("T0",))
            S.op("pool", lambda e, sinb=sinb: e.tensor_tensor(out=T[1][:], in0=x1, in1=sinb, op=ALU.mult), ("JUNK", "SIN"), ("T1",))
            S.op("pool", lambda e: e.tensor_tensor(out=QR3[:, :, 32:64], in0=T[0][:], in1=T[1][:], op=ALU.add),
                 ("T0", "T1"), ("QR",))
            for j in range(6 if kv_only else 0, 12):
                P = self.PT[0] if j < 6 else self.PT[1]
                pk = "PT0" if j < 6 else "PT1"
                jj = j % 6
                S.op("pe", lambda e, j=j, jj=jj, P=P: e.transpose(
                    out=P[:, jj * 128:(jj + 1) * 128], in_=self.QR[:, j * 128:(j + 1) * 128], identity=self.IDB[:]),
                     ("QR", "IDB"), (pk,))
            if not kv_only:
                S.op("act", lambda e: e.copy(out=self.QTS[:], in_=self.PT[0][:, 0:768]), ("PT0",), ("QTS",))
            S.op("act", lambda e: e.copy(out=self.KTS[:], in_=self.PT[1][:, 0:768]), ("PT1",), ("KTS",))
            qtd = self.qtd[:, :, tt * 128:(tt + 1) * 128].rearrange("h p t -> p h t")
            ktd = kv["kt"][:, :, tt * 128:(tt + 1) * 128].rearrange("h p t -> p h t")
            if not kv_only:
                S.dma("sync", lambda e, qtd=qtd: e.dma_start(out=qtd, in_=self.QTS[:].rearrange("p (h t) -> p h t", t=128)),
                      ("QTS",), ("qtd",))
            S.dma("sync", lambda e, ktd=ktd: e.dma_start(out=ktd, in_=self.KTS[:].rearrange("p (h t) -> p h t", t=128)),
                  ("KTS",), ("ktd_" + own,))
            S.op("act", lambda e: e.copy(out=self.VES[:, :, 0:64],
                                         in_=self.PROJ[:, 1536:2304].rearrange("p (h d) -> p h d", d=64)),
                 ("PROJ",), ("VES",))
            ved = kv["ve"][:, tt, :, :].rearrange("h p f -> p h f")
            S.dma("sync", lambda e, ved=ved: e.dma_start(out=ved, in_=self.VES[:].rearrange("p (h e) f -> p h (e f)", e=2)),
                  ("VES",), ("ved_" + own,))
            if kv_only:
                if tt == NT - 1:
                    S.op("dve", lambda e: e.tensor_tensor(out=self.UC[:], in0=self.PROJ[:, 2560:2816],
                                                          in1=self.PROJ[:, 2816:3072], op=ALU.mult), ("PROJ",), ("UC",))
                    S.dma("sync", lambda e: e.dma_start(out=kv["u"], in_=self.UC[126:128, :]), ("UC",), ("u_" + own,))
                continue
            uw = self.UW[tt % 2] if tt > 0 else self.UW0
            uwk = ("UW%d" % (tt % 2)) if tt > 0 else "UW0"
            S.op("dve", lambda e: e.tensor_tensor(out=self.UC[:], in0=self.PROJ[:, 2560:2816], in1=self.PROJ[:, 2816:3072],
                                                  op=ALU.mult), ("PROJ",), ("UC",))
            S.op("pool", lambda e, uw=uw: e.tensor_tensor(
                out=uw[:].rearrange("p (k c) -> p k c", k=3),
                in0=self.UC[:].unsqueeze(1).to_broadcast([128, 3, 256]),
                in1=self.CONVW[:].rearrange("p (k c) -> p k c", k=3), op=ALU.mult), ("UC", "CONVW"), (uwk,))
            if tt == NT - 1:
                S.dma("sync", lambda e: e.dma_start(out=kv["u"], in_=self.UC[126:128, :]), ("UC",), ("u_" + own,))
            if tt == 0:
                S.op("act", lambda e: e.copy(out=self.B0[:], in_=self.PROJ[:, 2304:2560]), ("PROJ",), ("B0",))
            else:
                prev = self.UW[(tt - 1) % 2] if tt > 1 else self.UW0
                pk_ = ("UW%d" % ((tt - 1) % 2)) if tt > 1 else "UW0"
                self._conv_mm(uw, uwk, prev, pk_)
                S.op("dve", lambda e, tt=tt: e.tensor_tensor(out=self.MIX[:, tt, 768:1024], in0=self.PB[0][:, 0:256],
                                                             in1=self.PROJ[:, 2304:2560], op=ALU.mult),
                     ("PB0", "PROJ"), ("MIX",))

    def _conv_mm(self, uw, uwk, prev, pk_):
        S = self.S
        Y = self.PB[0][:, 0:256]
        terms = [(0, uw, uwk, 0), (1, uw, uwk, 1), (None, uw, uwk, 2), (2, prev, pk_, 0), (3, prev, pk_, 1)]
        for i, (m, src, sk, k) in enumerate(terms):
            lhs = self.IDF[:] if m is None else self.SHM[:, m, :]
            S.op("pe", lambda e, lhs=lhs, src=src, k=k, i=i: e.matmul(
                Y, lhsT=lhs, rhs=src[:, k * 256:(k + 1) * 256], start=(i == 0), stop=(i == 4)),
                 (sk, "SHM", "IDF"), ("PB0",))

    def _step_B(self, l, own, halo, xname, fc, cvt=None, peer=True):
        S = self.S
        xs_ = self.xs[xname]
        xkey = "xs_" + xname
        kvo = self.kvset(own)
        kvh = self.kvset(halo)
        S.barrier()
        if cvt is not None:
            self._step_CVT(cvt)
        S.dma("sync", lambda e: e.dma_start(out=self.MASK[:], in_=self.c_mask), (), ("MASK",))
        S.op("pool", lambda e: e.memset(self.UH[:], 0.0), (), ("UH",))
        S.dma("sync", lambda e: e.dma_start(out=self.UH[126:128, :], in_=kvh["u"]), ("u_" + halo,), ("UH",))
        S.op("dve", lambda e: e.tensor_scalar(out=self.UH[:], in0=self.UH[:], scalar1=self.FLAG[:, fc:fc + 1], scalar2=None,
                                              op0=ALU.mult), ("UH", "FLAG"), ("UH",))
        S.op("pool", lambda e: e.tensor_tensor(
            out=self.UWH[:].rearrange("p (k c) -> p k c", k=3),
            in0=self.UH[:].unsqueeze(1).to_broadcast([128, 3, 256]),
            in1=self.CONVW[:].rearrange("p (k c) -> p k c", k=3), op=ALU.mult), ("UH", "CONVW"), ("UWH",))
        self._conv_mm(self.UW0, "UW0", self.UWH, "UWH")
        S.op("dve", lambda e: e.tensor_tensor(out=self.MIX[:, 0, 768:1024], in0=self.PB[0][:, 0:256], in1=self.B0[:],
                                              op=ALU.mult), ("PB0", "B0"), ("MIX",))
        for hp in range(6):
            S.dma("sync", lambda e, hp=hp: e.dma_start(out=self.KT[:, 0:TPC], in_=kvh["kt"][hp]),
                  ("ktd_" + halo,), ("KTh",))
            S.dma("sync", lambda e, hp=hp: e.dma_start(out=self.KT[:, TPC:2 * TPC], in_=kvo["kt"][hp]),
                  ("ktd_" + own,), ("KTo",))
            S.dma("sync", lambda e, hp=hp: e.dma_start(out=self.QT[:], in_=self.qtd[hp]), ("qtd",), ("QT",))
            S.dma("sync", lambda e, hp=hp: e.dma_start(out=self.VE[:, 0:NT, :],
                                                       in_=kvh["ve"][hp].rearrange("b p f -> p b f")),
                  ("ved_" + halo,), ("VEh",))
            S.dma("sync", lambda e, hp=hp: e.dma_start(out=self.VE[:, NT:2 * NT, :],
                                                       in_=kvo["ve"][hp].rearrange("b p f -> p b f")),
                  ("ved_" + own,), ("VEo",))
            S.op("pool", lambda e: e.tensor_scalar(out=self.VE[:, 0:NT, :], in0=self.VE[:, 0:NT, :],
                                                   scalar1=self.FLAG[:, fc:fc + 1], scalar2=None, op0=ALU.mult),
                 ("VEh", "FLAG"), ("VEh",))
            its = []
            for e2 in range(2):
                for g in range(4):
                    kbs = [kb for kb in range(g * 4, g * 4 + 20)
                           if any(0 <= (16 + g * 4) - kb + m <= 16 for m in range(4))]
                    for kb in kbs:
                        its.append((e2, g, kb, kb == kbs[-1]))
            LA = 2

            def front(it, e2, g, kb):
                p0, p1 = e2 * 64, (e2 + 1) * 64
                d0 = (16 + g * 4) - kb
                SB_, sk = self.SBK[it % 4]
                pe_, pek = self.PE_[it % 4], "PE%d" % (it % 4)
                pm, pmk = self.PM[it % 4], "PM%d" % (it % 4)
                S.op("pe", lambda e: e.matmul(
                    SB_, lhsT=self.KT[p0:p1, kb * 128:(kb + 1) * 128],
                    rhs=self.QT[p0:p1, g * 512:(g + 1) * 512], start=True, stop=True),
                     ("KTh" if kb < NT else "KTo", "QT"), (sk,))
                S.op("act", lambda e: e.activation(out=pe_, in_=SB_, func=AF.Exp, scale=0.125), (sk,), (pek,))
                mc = (d0 + 3) * 128
                S.op("dve", lambda e: e.tensor_tensor(out=pm, in0=pe_, in1=self.MASK[:, mc:mc + 512], op=ALU.mult),
                     (pek, "MASK"), (pmk,))

            def back(it, e2, g, kb, last):
                head = hp * 2 + e2
                d0 = (16 + g * 4) - kb
                pm, pmk = self.PM[it % 4], "PM%d" % (it % 4)
                for m in range(4):
                    dl = d0 + m
                    if not (0 <= dl <= 16):
                        continue
                    vkey = "VEh" if kb < NT else "VEo"
                    S.op("pe", lambda e, m=m, dl=dl: e.matmul(
                        self.PB[2 + m][:, 0:65], lhsT=pm[:, m * 128:(m + 1) * 128],
                        rhs=self.VE[:, kb, e2 * 65:(e2 + 1) * 65], start=(dl == 16), stop=(dl == 0)),
                         (pmk, vkey), ("PB%d" % (2 + m),))
                if last:
                    for m in range(4):
                        ok = "PB%d" % (2 + m)
                        S.op("dve", lambda e, m=m: e.reciprocal(out=self.RC[:, m:m + 1], in_=self.PB[2 + m][:, 64:65]),
                             (ok,), ("RC%d" % m,))
                        S.op("dve", lambda e, m=m: e.tensor_scalar(
                            out=self.MIX[:, g * 4 + m, head * 64:(head + 1) * 64], in0=self.PB[2 + m][:, 0:64],
                            scalar1=self.RC[:, m:m + 1], scalar2=None, op0=ALU.mult), (ok, "RC%d" % m), ("MIX",))

            for idx in range(len(its) + LA):
                if idx < len(its):
                    front(idx, *its[idx][:3])
                if idx - LA >= 0:
                    back(idx - LA, *its[idx - LA])
        S.barrier()
        self._load_w(self.w_out, l, 0, 1024, 0, "WBUF")
        self._load_w(self.w_q, l, 0, 2048, 1024, "WBUF")
        for j in range(2):
            S.dma("pool", lambda e, j=j: e.dma_start(out=self.KEYS[:, j * 1024:(j + 1) * 1024],
                                                     in_=self.keysT[l, :, j * 1024:(j + 1) * 1024]), (), ("KEYS%d" % j,))
        for tt in range(NT):
            xt = self.XT[tt % 2]
            xk = "XT%d" % (tt % 2)
            S.dma("sync", lambda e, xt=xt, tt=tt: e.dma_start(out=xt[:], in_=xs_[tt * 128:(tt + 1) * 128, :]),
                  (xkey,), (xk,))
            self._transpose8(lambda c, tt=tt: self.MIX[:, tt, c * 128:(c + 1) * 128], "MIX",
                             self.HT[:].rearrange("p a b -> p (a b)"), "HT", 0)
            for n in range(2):
                for c in range(8):
                    S.op("pe", lambda e, n=n, c=c: e.matmul(
                        self.PB[4 + n][:], lhsT=self.HT[:, c, :], rhs=self.WBUF[:, c, n * 512:(n + 1) * 512],
                        start=(c == 0), stop=(c == 7)), ("HT",) + tuple(self.wkeys[c]), ("PB%d" % (4 + n),))
                S.op("dve", lambda e, n=n: e.tensor_tensor(
                    out=self.HF[:, n * 512:(n + 1) * 512], in0=self.PB[4 + n][:],
                    in1=self.MODT[:, 2048 + n * 512:2048 + (n + 1) * 512], op=ALU.mult),
                     ("PB%d" % (4 + n), "MODT"), ("JUNK",))
            S.op("pool", lambda e, xt=xt: e.tensor_tensor(out=xt[:], in0=xt[:], in1=self.HF, op=ALU.add),
                 (xk, "JUNK"), (xk,))
            S.dma("sync", lambda e, xt=xt, tt=tt: e.dma_start(out=xs_[tt * 128:(tt + 1) * 128, :], in_=xt[:]),
                  (xk,), (xkey,))
        S.barrier()
        if peer:
            self._peer(l, xname)

    def _peer(self, l, xname):
        S = self.S
        xs_ = self.xs[xname]
        xkey = "xs_" + xname
        SM = self.SMALL
        def prologue(tt):
            par = tt % 2
            GATE, gk = self.GATEB[par], "GATE%d" % par
            EIDX, ek = self.EIDXB[par], "EIDX%d" % par
            HBD, hk = self.HBD[par], "HBD%d" % par
            xt = self.XT[tt % 2]
            xk = "XT%d" % (tt % 2)
            S.dma("sync", lambda e, xt=xt, tt=tt: e.dma_start(out=xt[:], in_=xs_[tt * 128:(tt + 1) * 128, :]),
                  (xkey,), (xk,))
            self._rmsnorm(xt, xk, 4096, 3072, True)
            S.op("act", lambda e: e.copy(out=HBD[:], in_=self.HF), ("JUNK",), (hk,))
            self._transpose8(lambda c: self.HB[:, c * 128:(c + 1) * 128], "HB",
                             self.HT[:].rearrange("p a b -> p (a b)"), "HT", 0)
            for n in range(4):
                for c in range(8):
                    S.op("pe", lambda e, n=n, c=c: e.matmul(
                        self.PB[n][:], lhsT=self.HT[:, c, :], rhs=self.WBUF[:, c, 1024 + n * 512:1024 + (n + 1) * 512],
                        start=(c == 0), stop=(c == 7)), ("HT",) + tuple(self.wkeys[c]), ("PB%d" % n,))
                S.op("act", lambda e, n=n: e.copy(out=self.QP[:, n * 512:(n + 1) * 512], in_=self.PB[n][:]),
                     ("PB%d" % n,), ("QP",))
            for half in range(2):
                self._transpose8(lambda c, half=half: self.QP[:, (half * 8 + c) * 128:(half * 8 + c + 1) * 128], "QP",
                                 self.QPT[:, half * 8:(half + 1) * 8, :].rearrange("p a b -> p (a b)"), "QPT", half)
            for j in range(16):
                S.op("pe", lambda e, j=j: e.matmul(
                    self.PB[j // 4][:, (j % 4) * 128:(j % 4 + 1) * 128], lhsT=self.QPT[:, j, :],
                    rhs=self.KEYS[:, j * 128:(j + 1) * 128], start=True, stop=True), ("QPT", "KEYS0", "KEYS1"), ("PB%d" % (j // 4),))
            for n in range(4):
                S.op("act", lambda e, n=n: e.copy(out=self.SC[:, n * 4:(n + 1) * 4, :].rearrange("p a b -> p (a b)"),
                                                  in_=self.PB[n][:]), ("PB%d" % n,), ("SC",))
            for j in range(16):
                S.op("dve", lambda e, j=j: e.max(out=self.V16[:, j, 0:8], in_=self.SC[:, j, :]), ("SC",), ("V16",))
                S.op("dve", lambda e, j=j: e.match_replace(out=self.SCM[:, 0:128], in_to_replace=self.V16[:, j, 0:8],
                                                           in_values=self.SC[:, j, :], imm_value=-1e30),
                     ("SC", "V16"), ("SCM",))
                S.op("dve", lambda e, j=j: e.max(out=self.V16[:, j, 8:16], in_=self.SCM[:, 0:128]), ("SCM",), ("V16",))
                S.op("dve", lambda e, j=j: e.max_index(out=self.IX[:, j, 0:8], in_max=self.V16[:, j, 0:8],
                                                       in_values=self.SC[:, j, :]), ("SC", "V16"), ("IX",))
                S.op("dve", lambda e, j=j: e.max_index(out=self.IX[:, j, 8:16], in_max=self.V16[:, j, 8:16],
                                                       in_values=self.SC[:, j, :]), ("SC", "V16"), ("IX",))
            S.op("dve", lambda e: e.tensor_copy(out=self.IXF, in_=self.IX), ("IX",), ("IXF",))
            V4 = self.V16.rearrange("p (h s) r -> p h s r", s=2)
            I4 = self.IXF.rearrange("p (h s) r -> p h s r", s=2)
            C4 = self.CAND.rearrange("p h (a b) -> p h a b", b=16)
            S.op("dve", lambda e: e.tensor_tensor(
                out=C4, in0=V4[:, :, 0, :].unsqueeze(3).to_broadcast([128, 8, 16, 16]),
                in1=V4[:, :, 1, :].unsqueeze(2).to_broadcast([128, 8, 16, 16]), op=ALU.add), ("V16",), ("CAND",))
            for h in range(8):
                S.op("dve", lambda e, h=h: e.max(out=self.BEST[:, h, 0:8], in_=self.CAND[:, h, :]), ("CAND",), ("BEST",))
                S.op("dve", lambda e, h=h: e.match_replace(out=self.SCM, in_to_replace=self.BEST[:, h, 0:8],
                                                           in_values=self.CAND[:, h, :], imm_value=-1e30),
                     ("CAND", "BEST"), ("SCM",))
                S.op("dve", lambda e, h=h: e.max(out=self.BEST[:, h, 8:16], in_=self.SCM), ("SCM",), ("BEST",))
                S.op("dve", lambda e, h=h: e.max_index(out=self.POS[:, h, 0:8], in_max=self.BEST[:, h, 0:8],
                                                       in_values=self.CAND[:, h, :]), ("CAND", "BEST"), ("POS",))
                S.op("dve", lambda e, h=h: e.max_index(out=self.POS[:, h, 8:16], in_max=self.BEST[:, h, 8:16],
                                                       in_values=self.CAND[:, h, :]), ("CAND", "BEST"), ("POS",))
            S.op("dve", lambda e: e.tensor_tensor(out=GATE, in0=self.BEST,
                                                  in1=self.BEST[:, :, 0:1].to_broadcast([128, 8, 16]), op=ALU.subtract),
                 ("BEST",), (gk,))
            S.op("act", lambda e: e.activation(out=GATE, in_=GATE, func=AF.Exp), (gk,), (gk,))
            S.op("dve", lambda e: e.tensor_reduce(out=SM[:, 56:64], in_=GATE, axis=AX.X, op=ALU.add),
                 (gk,), ("SM56",))
            S.op("dve", lambda e: e.reciprocal(out=SM[:, 56:64], in_=SM[:, 56:64]), ("SM56",), ("SM56",))
            S.op("dve", lambda e: e.tensor_tensor(out=GATE, in0=GATE,
                                                  in1=SM[:, 56:64].unsqueeze(2).to_broadcast([128, 8, 16]), op=ALU.mult),
                 (gk, "SM56"), (gk,))
            S.op("dve", lambda e: e.tensor_single_scalar(out=self.R0, in_=self.POS, scalar=4,
                                                         op=ALU.logical_shift_right), ("POS",), ("R0",))
            S.op("dve", lambda e: e.tensor_single_scalar(out=self.R1, in_=self.POS, scalar=15,
                                                         op=ALU.bitwise_and), ("POS",), ("R1",))
            S.op("dve", lambda e: e.tensor_copy(out=self.R0F, in_=self.R0), ("R0",), ("R0F",))
            S.op("dve", lambda e: e.tensor_copy(out=self.R1F, in_=self.R1), ("R1",), ("R1F",))
            iob = self.IOTA16[:].unsqueeze(1).unsqueeze(1).to_broadcast([128, 8, 16, 16])
            for (rf, rk, side, dst, dk) in ((self.R0F, "R0F", 0, self.ISEL, "ISEL"), (self.R1F, "R1F", 1, self.JSEL, "JSEL")):
                S.op("dve", lambda e, rf=rf: e.tensor_tensor(
                    out=self.OH, in0=iob, in1=rf.unsqueeze(3).to_broadcast([128, 8, 16, 16]), op=ALU.is_equal),
                     (rk, "IOTA16"), ("OH",))
                S.op("dve", lambda e, side=side: e.tensor_tensor(
                    out=self.OH, in0=self.OH, in1=I4[:, :, side, :].unsqueeze(2).to_broadcast([128, 8, 16, 16]),
                    op=ALU.mult), ("OH", "IXF"), ("OH",))
                S.op("dve", lambda e, dst=dst: e.tensor_reduce(out=dst, in_=self.OH, axis=AX.X, op=ALU.add),
                     ("OH",), (dk,))
            S.op("dve", lambda e: e.scalar_tensor_tensor(out=self.ISEL, in0=self.ISEL, scalar=128.0, in1=self.JSEL,
                                                         op0=ALU.mult, op1=ALU.add), ("ISEL", "JSEL"), ("ISEL",))
            if l > 0:
                S.op("dve", lambda e: e.tensor_scalar(out=self.ISEL, in0=self.ISEL, scalar1=float(l * NEXP), scalar2=None,
                                                      op0=ALU.add), ("ISEL",), ("ISEL",))
            S.op("dve", lambda e: e.tensor_copy(out=EIDX, in_=self.ISEL.rearrange("p h r -> p (h r)")),
                 ("ISEL",), (ek,))

        def loop(tt, pending):
            xt = self.XT[tt % 2]
            xk = "XT%d" % (tt % 2)
            par = tt % 2
            GATE, gk = self.GATEB[par], "GATE%d" % par
            EIDX, ek = self.EIDXB[par], "EIDX%d" % par
            HBD, hk = self.HBD[par], "HBD%d" % par
            uv2d = self.uvb.rearrange("l e d -> (l e) d")
            gflat = GATE.rearrange("p h r -> p (h r)")
            GS = 2
            nrec = len(pending)
            per_group = -(-nrec // (124 // GS)) if nrec else 0
            for gb in range(0, 128, GS):
                ak, zk = "ACTV%d_%d" % (tt, gb), "ZZ%d_%d" % (tt, gb)
                for k in range(gb, gb + GS):
                    uv, uvk = self.UVG[k % 8], "UVG%d" % (k % 8)
                    S.dma("pool", lambda e, uv=uv, k=k: e.indirect_dma_start(
                        out=uv, out_offset=None, in_=uv2d,
                        in_offset=bass.IndirectOffsetOnAxis(ap=EIDX[:, k:k + 1], axis=0)), (ek,), (uvk,))
                    S.op("dve", lambda e, uv=uv, k=k: e.scalar_tensor_tensor(
                        out=self.QR[:, 0:1024], in0=uv[:, 0:1024], scalar=1.0, in1=HBD[:], op0=ALU.mult, op1=ALU.mult,
                        accum_out=self.ACTV[:, k:k + 1]), (uvk, hk), (ak,))
                S.op("act", lambda e, gb=gb: e.activation(out=self.ZZ[:, gb:gb + GS], in_=self.ACTV[:, gb:gb + GS],
                                                         func=AF.Gelu), (ak,), (zk,))
                for k in range(gb, gb + GS):
                    S.op("act", lambda e, k=k: e.activation(out=self.ZZ[:, k:k + 1], in_=self.ZZ[:, k:k + 1], func=AF.Copy,
                                                            scale=gflat[:, k:k + 1]), (zk, gk), (zk,))
                for k in range(gb, gb + GS):
                    uv, uvk = self.UVG[k % 8], "UVG%d" % (k % 8)
                    dk, dkk = self.DK[k % 4], "DK%d" % (k % 4)
                    S.op("act", lambda e, dk=dk, k=k: e.activation(out=dk, in_=self.IDF[:], func=AF.Copy,
                                                                    scale=self.ZZ[:, k:k + 1]), ("IDF", zk), (dkk,))
                    for n in range(2):
                        S.op("pe", lambda e, dk=dk, uv=uv, n=n, k=k: e.matmul(
                            self.PB[4 + n][:], lhsT=dk, rhs=uv[:, 1024 + n * 512:1024 + (n + 1) * 512],
                            start=(k == 0), stop=(k == 127)), (dkk, uvk), ("PB%d" % (4 + n),))
                if pending:
                    S.replay_records(pending[:per_group])
                    del pending[:per_group]
            if pending:
                S.replay_records(pending)
                del pending[:]

        def epilogue(tt):
            xt = self.XT[tt % 2]
            xk = "XT%d" % (tt % 2)
            for n in range(2):
                S.op("dve", lambda e, n=n: e.tensor_tensor(
                    out=self.HF[:, n * 512:(n + 1) * 512], in0=self.PB[4 + n][:],
                    in1=self.MODT[:, 5120 + n * 512:5120 + (n + 1) * 512], op=ALU.mult),
                     ("PB%d" % (4 + n), "MODT"), ("JUNK",))
            S.op("pool", lambda e, xt=xt: e.tensor_tensor(out=xt[:], in0=xt[:], in1=self.HF, op=ALU.add),
                 (xk, "JUNK"), (xk,))
            S.dma("sync", lambda e, xt=xt, tt=tt: e.dma_start(out=xs_[tt * 128:(tt + 1) * 128, :], in_=xt[:]),
                  (xk,), (xkey,))


        S.wait_cvt("pool")
        prologue(0)
        for tt in range(NT):
            pending = []
            if tt + 1 < NT:
                S.rec = []
                prologue(tt + 1)
                pending = S.rec
                S.rec = None
            loop(tt, pending)
            epilogue(tt)

    def _finish(self):
        S = self.S
        S.barrier()
        if self.x_out is not None:
            S.dma("sync", lambda e: e.dma_start(out=self.x_out, in_=self.xs["c0"]), ("xs_c0",), ("x_out",))
        S.barrier()


def _build(plan, ext):
    b = Builder(plan, ext)
    nc = b.build()
    return nc, list(b.used_inputs)


def _common_inputs(x, c, positions, w_ada, b_ada, norm_mix, norm_ffn, w_in, q_norm, k_norm, conv_w, w_out,
                   peer_wq, peer_keys, peer_u, peer_v):
    f = np.float32
    cons = _consts()
    cvec = np.asarray(c, f).reshape(8, 128)
    cB = np.ascontiguousarray(np.broadcast_to(cvec.T[:, :, None], (128, 8, 128))).astype(f)
    brow = np.concatenate([np.asarray(b_ada, f), np.asarray(norm_mix, f), np.asarray(norm_ffn, f)], axis=1)
    brow = np.ascontiguousarray(np.broadcast_to(brow[:, None, :], (2, 128, 8192)))
    qk = np.concatenate([np.asarray(q_norm, f), np.asarray(k_norm, f)], axis=1)
    qkg = np.ascontiguousarray(np.broadcast_to(qk[:, None, :], (2, 128, 128)))
    cw = np.asarray(conv_w, f).reshape(2, 768)
    convw = np.ascontiguousarray(np.broadcast_to(cw[:, None, :], (2, 128, 768)))
    keysT = np.ascontiguousarray(np.asarray(peer_keys, f).transpose(0, 4, 1, 2, 3).reshape(2, 128, 2048))
    shared = dict(cB=cB, w_ada=np.asarray(w_ada, f), brow=brow, w_in=np.asarray(w_in, f), w_out=np.asarray(w_out, f),
                  w_q=np.asarray(peer_wq, f), qkg=qkg, convw=convw, keysT=keysT, peer_u=np.asarray(peer_u, f),
                  peer_v=np.asarray(peer_v, f), identb=cons["identb"], identf=cons["identf"], mask=cons["mask"],
                  freq=cons["freq"], shm=cons["shm"], iota16=cons["iota16"])
    xs = np.asarray(x, f).reshape(NCORES, TPC, D)
    pos = np.asarray(positions).astype(np.int32).reshape(NCORES, NT, 128)
    zx = np.zeros((TPC, D), f)
    zp = np.zeros((NT, 128), np.int32)
    per = []
    for i in range(NCORES):
        d = dict(shared)
        d["x0"] = np.ascontiguousarray(xs[i])
        d["xm1"] = np.ascontiguousarray(xs[i - 1]) if i >= 1 else zx
        d["xm2"] = np.ascontiguousarray(xs[i - 2]) if i >= 2 else zx
        pp = [pos[i - 2] if i >= 2 else zp, pos[i - 1] if i >= 1 else zp, pos[i]]
        d["posi"] = np.ascontiguousarray(np.concatenate([p.T for p in pp], axis=1))
        d["flag"] = np.array([[1.0 if i >= 1 else 0.0, 1.0 if i >= 2 else 0.0]] * 128, f)
        per.append(d)
    return per


_PROGS = {}


def _prog(name, plan, ext):
    if name not in _PROGS:
        _PROGS[name] = _build(plan, ext)
    return _PROGS[name]


def _run(prog, per, extra=None):
    nc, used = prog
    maps = []
    for i in range(NCORES):
        d = {k: per[i][k] for k in used if k in per[i]}
        if extra is not None:
            d.update(extra[i])
        maps.append(d)
    return run_bass_kernel_spmd(nc, maps, core_ids=list(range(NCORES))).results


PLAN = [("MOD", 0),
        ("A", 0, "kvA", "xm2", 0, True),
        ("A", 0, "kvB", "m1", 1), ("B", 0, "kvB", "kvA", "m1", 1, 0),
        ("A", 0, "kvC", "c0", 2), ("B", 0, "kvC", "kvB", "c0", 0, 1),
        ("MOD", 1),
        ("A", 1, "kvD", "m1", 1, True),
        ("A", 1, "kvE", "c0", 2), ("B", 1, "kvE", "kvD", "c0", 0)]


def kernel(**inputs):
    per = _common_inputs(**inputs)
    prog = _prog("fused", PLAN, {"x_out": "out"})
    r = _run(prog, per)
    out = np.stack([r[i]["x_out"] for i in range(NCORES)], axis=0).reshape(1, SEQ, D)
    return out.astype(np.float32)
```

```python
import math
from contextlib import ExitStack
import numpy as np
import ml_dtypes
import concourse.bass as bass
import concourse.mybir as mybir
from concourse.bass_utils import run_bass_kernel_spmd

F32 = mybir.dt.float32
BF16 = mybir.dt.bfloat16
I32 = mybir.dt.int32
U32 = mybir.dt.uint32
AF = mybir.ActivationFunctionType
ALU = mybir.AluOpType
AX = mybir.AxisListType

NCORES = 8
SEQ = 16384
D = 1024
TPC = SEQ // NCORES
NT = TPC // 128
EPS = 1e-6
NEXP = 16384


class Sched:
    COMPUTE = ("act", "dve", "pool", "pe")
    NDMA = 24

    def __init__(self, nc, stack):
        self.nc = nc
        self.ops = {e: [] for e in ("sync",) + self.COMPUTE}
        self.sem = {e: stack.enter_context(nc.semaphore("s_" + e)) for e in self.COMPUTE}
        self.cnt = {e: 0 for e in self.COMPUTE}
        self.dsem = [stack.enter_context(nc.semaphore("d%d" % i)) for i in range(self.NDMA)]
        self.dcnt = [0] * self.NDMA
        self.drr = 0
        self.waited = {e: {} for e in self.ops}
        self.last_w = {}
        self.readers = {}
        self.all_events = {}
        self.rec = None
        self.csem = [stack.enter_context(nc.semaphore("cv%d" % i)) for i in range(8)]
        self.ccnt = [0] * 8
        self.crr = 0
        self.cvt_events = {}

    def _collect(self, eng, reads, writes):
        evs = []
        for k in reads:
            if k in self.last_w:
                evs.append(self.last_w[k])
        for k in writes:
            if k in self.last_w:
                evs.append(self.last_w[k])
            evs.extend(self.readers.get(k, ()))
        out = []
        for (sid, sem, val, e) in evs:
            if eng == "pe" and e == "pe":
                continue
            if self.waited[eng].get(sid, 0) >= val:
                continue
            self.waited[eng][sid] = val
            out.append((sem, val))
        return out

    def _record(self, ev, reads, writes):
        for k in reads:
            self.readers.setdefault(k, []).append(ev)
        for k in writes:
            self.last_w[k] = ev
            self.readers[k] = []
        self.all_events[ev[0]] = ev

    def op(self, eng, fn, reads=(), writes=()):
        if self.rec is not None:
            self.rec.append(("op", eng, fn, tuple(reads), tuple(writes)))
            return
        waits = self._collect(eng, reads, writes)
        self.cnt[eng] += 1
        ev = ("c_" + eng, self.sem[eng], self.cnt[eng], eng)
        self.ops[eng].append((waits, fn, (self.sem[eng], 1)))
        self._record(ev, reads, writes)

    def dma(self, queue, fn, reads=(), writes=()):
        if self.rec is not None:
            self.rec.append(("dma", queue, fn, tuple(reads), tuple(writes)))
            return
        j = self.drr
        self.drr = (j + 1) % self.NDMA
        waits = self._collect(queue, reads, writes)
        sid = "d%d" % j
        if self.dcnt[j] > 0 and self.waited[queue].get(sid, 0) < self.dcnt[j]:
            waits.append((self.dsem[j], self.dcnt[j]))
            self.waited[queue][sid] = self.dcnt[j]
        self.dcnt[j] += 16
        ev = (sid, self.dsem[j], self.dcnt[j], "dma")
        self.ops[queue].append((waits, fn, (self.dsem[j], 16)))
        self._record(ev, reads, writes)

    def dma_cvt(self, queue, fn):
        j = self.crr
        self.crr = (j + 1) % 8
        waits = []
        sid = "cv%d" % j
        if self.ccnt[j] > 0 and self.waited[queue].get(sid, 0) < self.ccnt[j]:
            waits.append((self.csem[j], self.ccnt[j]))
            self.waited[queue][sid] = self.ccnt[j]
        self.ccnt[j] += 16
        self.cvt_events[sid] = (sid, self.csem[j], self.ccnt[j], "dma")
        self.ops[queue].append((waits, fn, (self.csem[j], 16)))

    def wait_cvt(self, queue):
        waits = []
        for (sid, sem, val, e) in self.cvt_events.values():
            if self.waited[queue].get(sid, 0) >= val:
                continue
            self.waited[queue][sid] = val
            waits.append((sem, val))
        if waits:
            self.ops[queue].append((waits, None, None))

    def replay_records(self, recs):
        for (kind, eng, fn, reads, writes) in recs:
            (self.op if kind == "op" else self.dma)(eng, fn, reads, writes)

    def barrier(self):
        evs = list(self.all_events.values())
        for eng in self.ops:
            waits = []
            for (sid, sem, val, e) in evs:
                if self.waited[eng].get(sid, 0) >= val:
                    continue
                self.waited[eng][sid] = val
                waits.append((sem, val))
            if waits:
                self.ops[eng].append((waits, None, None))

    def emit(self):
        nc = self.nc
        with nc.Block() as block:
            def replay(name):
                def run(eng):
                    for (waits, fn, inc) in self.ops[name]:
                        for (sem, val) in waits:
                            eng.wait_ge(sem, val)
                        if fn is not None:
                            ins = fn(eng)
                            ins.then_inc(inc[0], inc[1])
                return run
            block.sync(replay("sync"))
            block.scalar(replay("act"))
            block.vector(replay("dve"))
            block.gpsimd(replay("pool"))
            block.tensor(replay("pe"))


def _mask_table():
    kk = np.arange(128)[:, None]
    qq = np.arange(128)[None, :]
    blocks = []
    for delta in range(-3, 20):
        d = delta * 128 + qq - kk
        m = ((d >= 0) & (d <= 128)).astype(np.float32)
        m += ((d >= 0) & (d <= 512) & (d % 4 == 0)).astype(np.float32)
        m += ((d >= 0) & (d <= 2048) & (d % 16 == 0)).astype(np.float32)
        blocks.append(m)
    return np.concatenate(blocks, axis=1).astype(ml_dtypes.bfloat16)


def _consts():
    c = {}
    c["identb"] = np.eye(128, dtype=np.float32).astype(ml_dtypes.bfloat16)
    c["identf"] = np.eye(128, dtype=np.float32)
    c["mask"] = _mask_table()
    half = 32
    freq = (10000.0 ** (-np.arange(half, dtype=np.float32) / half)).astype(np.float32)
    c["freq"] = np.broadcast_to(freq[None, :], (128, half)).copy()
    sh = np.stack([np.eye(128, k=2), np.eye(128, k=1), np.eye(128, k=-126), np.eye(128, k=-127)])
    c["shm"] = np.ascontiguousarray(sh.transpose(1, 0, 2)).astype(np.float32)
    c["iota16"] = np.broadcast_to(np.arange(16, dtype=np.float32)[None, :], (128, 16)).copy()
    return c


class Builder:
    def __init__(self, plan, ext):
        self.plan = plan
        self.ext = ext
        self.nc = bass.Bass("TRN2", target_bir_lowering=False)
        self.st = ExitStack()
        self.S = None
        self.wkeys = [[] for _ in range(8)]

    def dram(self, name, shape, dt, kind=None):
        if kind is None:
            k = self.ext.get(name)
            kind = {"in": "ExternalInput", "out": "ExternalOutput", None: "Internal"}[k]
        return self.nc.dram_tensor(name, list(shape), dt, kind=kind).ap()

    def sb(self, name, shape, dt):
        return self.st.enter_context(self.nc.sbuf_tensor(name, list(shape), dt))

    def ps(self, name, shape, dt):
        return self.st.enter_context(self.nc.psum_tensor(name, list(shape), dt))

    def build(self):
        nc = self.nc
        with self.st:
            self.S = Sched(nc, self.st)
            self._declare()
            self._setup()
            for step in self.plan:
                getattr(self, "_step_" + step[0])(*step[1:])
            self._finish()
            self.S.emit()
        return nc

    IN_SPECS = {
        "x_m2": ("xm2", [TPC, D], F32), "x_m1": ("xm1", [TPC, D], F32), "x_0": ("x0", [TPC, D], F32), "cB": ("cB", [128, 8, 128], F32), "posi": ("posi", [128, 3 * NT], I32),
        "w_ada": ("w_ada", [2, D, 6144], F32), "brow": ("brow", [2, 128, 8192], F32),
        "w_in": ("w_in", [2, D, 3072], F32), "w_out": ("w_out", [2, D, D], F32), "w_q": ("w_q", [2, D, 2048], F32),
        "qkg": ("qkg", [2, 128, 128], F32), "convw": ("convw", [2, 128, 768], F32),
        "keysT": ("keysT", [2, 128, 2048], F32), "p_u": ("peer_u", [2, NEXP, D], F32),
        "p_v": ("peer_v", [2, NEXP, D], F32), "c_identb": ("identb", [128, 128], BF16),
        "c_identf": ("identf", [128, 128], F32), "c_mask": ("mask", [128, 23 * 128], BF16),
        "c_freq": ("freq", [128, 32], F32), "c_shm": ("shm", [128, 4, 128], F32),
        "c_iota16": ("iota16", [128, 16], F32), "c_flag": ("flag", [128, 2], F32),
    }

    def __getattr__(self, attr):
        specs = type(self).IN_SPECS
        if attr in specs:
            name, shape, dt = specs[attr]
            ap = self.dram(name, shape, dt, "ExternalInput")
            self.__dict__[attr] = ap
            self.__dict__.setdefault("used_inputs", []).append(name)
            return ap
        raise AttributeError(attr)

    def _declare(self):
        self.used_inputs = self.__dict__.get("used_inputs", [])
        self.xs = {"m1": self.dram("xs_m1", [TPC, D], F32, "Internal"), "c0": self.dram("xs_c0", [TPC, D], F32, "Internal")}
        self.qtd = self.dram("qtd", [6, 128, TPC], BF16, "Internal")
        self.uvb = self.dram("uvb", [2, NEXP, 2 * D], BF16, "Internal")
        self.x_out = self.dram("x_out", [TPC, D], F32) if self.ext.get("x_out") else None
        self.kv = {}

    def kvset(self, name):
        if name not in self.kv:
            self.kv[name] = dict(
                kt=self.dram(name + "_kt", [6, 128, TPC], BF16),
                ve=self.dram(name + "_ve", [6, NT, 128, 130], BF16),
                u=self.dram(name + "_u", [2, 256], F32),
            )
        return self.kv[name]

    def _setup(self):
        S = self.S
        sb, ps = self.sb, self.ps
        self.PB = [ps("pb%d" % i, [128, 512], F32) for i in range(6)]
        self.PT = [ps("pt%d" % i, [128, 1024], BF16) for i in range(2)]
        self.IDB = sb("idb", [128, 128], BF16)
        self.IDF = sb("idf", [128, 128], F32)
        self.MASK = sb("maskt", [128, 23 * 128], BF16)
        self.FREQ = sb("freqt", [128, 32], F32)
        self.SHM = sb("shmt", [128, 4, 128], F32)
        self.IOTA16 = sb("iota16t", [128, 16], F32)
        self.FLAG = sb("flagt", [128, 2], F32)
        self.POSI = sb("posit", [128, 3 * NT], I32)
        self.POSF = sb("posft", [128, 3 * NT], F32)
        self.COS = sb("cost", [128, NT, 32], F32)
        self.SIN = sb("sint", [128, NT, 32], F32)
        self.MODT = sb("modt", [128, 6144], F32)
        self.WBUF = sb("wbuf", [128, 8, 3072], BF16)
        self.MIX = sb("mixcat", [128, NT, 1024], BF16)
        self.PROJ = sb("proj", [128, 3072], F32)
        self.CB = self.PROJ[:, 0:1024].rearrange("p (a b) -> p a b", b=128)
        self.XT = [sb("xt%d" % i, [128, 1024], F32) for i in range(2)]
        self.HB = sb("hb", [128, 1024], BF16)
        self.HT = sb("ht", [128, 8, 128], BF16)
        self.JUNK = sb("junk", [128, 1536], F32)
        self.HF = self.JUNK[:, 0:1024]
        self.SMALL = sb("small", [128, 64], F32)
        self.QKG = sb("qkgt", [128, 128], F32)
        self.HBD = [sb("hbd%d" % i, [128, 1024], BF16) for i in range(2)]
        eg2 = sb("eg2", [128, 256], F32)
        self.EIDXB = [None, eg2[:, 0:128].bitcast(I32)]
        self.GATEB = [None, eg2[:, 128:256].rearrange("p (a b) -> p a b", b=16)]
        self.CONVW = sb("convwt", [128, 768], F32)
        self.TR = [sb("tr%d" % i, [128, 24, 32], F32) for i in range(3)]
        self.ANG = self.TR[0][:, 0:NT, :]
        self.ANG2 = self.TR[1][:, 0:NT, :]
        self.KR = self.TR[2][:, 0:NT, :]
        self.QR = sb("qr", [128, 1536], BF16)
        self.QTS = sb("qts", [128, 768], BF16)
        self.KTS = sb("kts", [128, 768], BF16)
        self.VES = sb("ves", [128, 12, 65], BF16)
        self.UC = sb("uc", [128, 256], F32)
        self.UW = [sb("uw%d" % i, [128, 768], F32) for i in range(2)]
        self.UW0 = sb("uwz", [128, 768], F32)
        self.B0 = sb("b0", [128, 256], F32)
        self.UH = sb("uh", [128, 256], F32)
        self.UWH = sb("uwh", [128, 768], F32)
        self.KT = sb("ktb", [128, 2 * TPC], BF16)
        self.QT = sb("qtb", [128, TPC], BF16)
        self.VE = sb("veb", [128, 2 * NT, 130], BF16)
        trb = [self.TR[i][:].rearrange("p a b -> p (a b)").bitcast(BF16) for i in range(3)]
        self.PM = [trb[0][:, 0:512], trb[0][:, 512:1024], trb[0][:, 1024:1536], trb[1][:, 0:512]]
        self.PE_ = [trb[1][:, 512:1024], trb[1][:, 1024:1536], trb[2][:, 0:512], trb[2][:, 512:1024]]
        self.SBK = [(self.PB[0][:], "PB0"), (self.PB[1][:], "PB1"),
                    (self.PT[0][:].bitcast(F32), "PT0"), (self.PT[1][:].bitcast(F32), "PT1")]
        self.RC = sb("rc", [128, 8], F32)
        self.SC = self.PROJ[:, 0:2048].rearrange("p (a b) -> p a b", b=128)
        self.QPT = self.PROJ[:, 2048:3072].bitcast(BF16).rearrange("p (a b) -> p a b", b=128)
        self.CAND = self.KT[:].bitcast(F32).rearrange("p (a b) -> p a b", b=256)
        self.OH = self.VE[:].rearrange("p a b -> p (a b)")[:, 0:4096].bitcast(F32).rearrange(
            "p (h a b) -> p h a b", a=16, b=16)
        self.QP = self.QT[:]
        self.KEYS = self.MASK[:, 0:2048]
        def carve(base, off, n, dt, pat=None, **kw):
            v = base[:, off:off + n]
            if dt is not F32:
                v = v.bitcast(dt)
            return v.rearrange(pat, **kw) if pat else v
        u0, u1, uz, uh_ = self.UW[0], self.UW[1], self.UW0, self.UWH
        self.SCM = carve(u0, 0, 256, F32)
        self.V16 = carve(u0, 256, 256, F32, "p (a b) -> p a b", b=16)
        self.IX = carve(u0, 512, 256, U32, "p (a b) -> p a b", b=16)
        self.IXF = carve(u1, 0, 256, F32, "p (a b) -> p a b", b=16)
        self.BEST = carve(u1, 256, 128, F32, "p (a b) -> p a b", b=16)
        self.POS = carve(u1, 384, 128, U32, "p (a b) -> p a b", b=16)
        self.R0 = carve(u1, 512, 128, U32, "p (a b) -> p a b", b=16)
        self.R1 = carve(u1, 640, 128, U32, "p (a b) -> p a b", b=16)
        self.R0F = carve(uz, 0, 128, F32, "p (a b) -> p a b", b=16)
        self.R1F = carve(uz, 128, 128, F32, "p (a b) -> p a b", b=16)
        self.ISEL = carve(uz, 256, 128, F32, "p (a b) -> p a b", b=16)
        self.JSEL = carve(uz, 384, 128, F32, "p (a b) -> p a b", b=16)
        self.EIDX = carve(uz, 512, 128, I32)
        self.GATE = carve(uz, 640, 128, F32, "p (a b) -> p a b", b=16)
        self.EIDXB[0] = self.EIDX
        self.GATEB[0] = self.GATE
        self.ACTV = carve(uh_, 0, 128, F32)
        self.ZZ = carve(uh_, 128, 128, F32)
        self.DK = [carve(uh_, 256 + 64 * i, 64, BF16) for i in range(4)]
        mixraw = self.MIX[:].rearrange("p a b -> p (a b)")
        self.WA = [mixraw[:, i * 6144:(i + 1) * 6144].bitcast(F32) for i in range(2)]
        self.UVG = [mixraw[:, i * 2048:(i + 1) * 2048] for i in range(8)]

        sy = lambda out, in_, r, w: S.dma("sync", lambda e: e.dma_start(out=out, in_=in_), r, w)
        sy(self.IDB[:], self.c_identb, (), ("IDB",))
        sy(self.IDF[:], self.c_identf, (), ("IDF",))
        sy(self.FREQ[:], self.c_freq, (), ("FREQ",))
        sy(self.SHM[:], self.c_shm, (), ("SHM",))
        sy(self.IOTA16[:], self.c_iota16, (), ("IOTA16",))
        sy(self.FLAG[:], self.c_flag, (), ("FLAG",))
        sy(self.POSI[:], self.posi, (), ("POSI",))
        S.dma("sync", lambda e: e.dma_start(out=self.xs["m1"], in_=self.x_m1), (), ("xs_m1",))
        S.dma("sync", lambda e: e.dma_start(out=self.xs["c0"], in_=self.x_0), (), ("xs_c0",))
        S.op("pool", lambda e: e.memset(self.VES[:], 1.0), (), ("VES",))
        S.op("dve", lambda e: e.tensor_copy(out=self.POSF[:], in_=self.POSI[:]), ("POSI",), ("POSF",))

    def _step_CVT(self, l):
        S = self.S
        for t, tab in enumerate((self.p_u, self.p_v)):
            for r0 in range(0, NEXP, 1024):
                S.dma_cvt("pool", lambda e, t=t, tab=tab, r0=r0: e.dma_start(
                    out=self.uvb[l, r0:r0 + 1024, t * D:(t + 1) * D], in_=tab[l, r0:r0 + 1024, :]))

    def xap(self, name):
        return self.x_m2 if name == "xm2" else self.xs[name]

    def _rope_tables(self, pc):
        S = self.S
        for tt in range(NT):
            S.op("dve", lambda e, tt=tt: e.tensor_scalar(
                out=self.ANG[:, tt, :], in0=self.FREQ[:], scalar1=self.POSF[:, pc * NT + tt:pc * NT + tt + 1],
                scalar2=None, op0=ALU.mult), ("FREQ", "POSF"), ("ANG",))
        S.op("dve", lambda e: e.tensor_scalar(out=self.ANG2, in0=self.ANG, scalar1=math.pi / 2, scalar2=None,
                                              op0=ALU.add), ("ANG",), ("ANG2",))
        MAGIC = 12582912.0
        C1 = 6.28125
        C2 = 2 * math.pi - 6.28125
        for (src, dst, key) in ((self.ANG, self.SIN, "SIN"), (self.ANG2, self.COS, "COS")):
            sk = "ANG" if src is self.ANG else "ANG2"
            S.op("dve", lambda e, src=src: e.tensor_scalar(out=self.KR, in0=src, scalar1=1.0 / (2 * math.pi),
                                                           scalar2=MAGIC, op0=ALU.mult, op1=ALU.add), (sk,), ("KR",))
            S.op("dve", lambda e: e.tensor_scalar(out=self.KR, in0=self.KR, scalar1=MAGIC, scalar2=None,
                                                  op0=ALU.subtract), ("KR",), ("KR",))
            S.op("dve", lambda e, src=src: e.scalar_tensor_tensor(out=src, in0=self.KR, scalar=-C1, in1=src,
                                                                  op0=ALU.mult, op1=ALU.add), ("KR", sk), (sk,))
            S.op("dve", lambda e, src=src: e.scalar_tensor_tensor(out=src, in0=self.KR, scalar=-C2, in1=src,
                                                                  op0=ALU.mult, op1=ALU.add), ("KR", sk), (sk,))
            S.op("act", lambda e, src=src, dst=dst: e.activation(out=dst[:], in_=src, func=AF.Sin), (sk,), (key,))
        S.op("dve", lambda e: e.tensor_copy(out=self.SMALL[:, 60:61], in_=self.SMALL[:, 60:61]), ("ANG", "ANG2", "KR"),
             ("T0", "T1", "T2"))

    def _rmsnorm(self, xt, xkey, a_off, sh_off, want_f32):
        S = self.S
        SM = self.SMALL
        S.op("act", lambda e: e.activation(out=self.HB[:], in_=xt[:], func=AF.Square,
                                           accum_out=SM[:, 0:1]), (xkey,), ("HB", "SM0"))
        S.op("act", lambda e: e.activation(out=SM[:, 1:2], in_=SM[:, 0:1], func=AF.Sqrt, bias=EPS, scale=1.0 / D),
             ("SM0",), ("SM1",))
        S.op("dve", lambda e: e.reciprocal(out=SM[:, 2:3], in_=SM[:, 1:2]), ("SM1",), ("SM2",))
        S.op("dve", lambda e: e.scalar_tensor_tensor(out=self.HF, in0=xt[:], scalar=SM[:, 2:3],
                                                     in1=self.MODT[:, a_off:a_off + 1024], op0=ALU.mult, op1=ALU.mult),
             (xkey, "SM2", "MODT"), ("JUNK",))
        if want_f32:
            S.op("pool", lambda e: e.tensor_tensor(out=self.HF, in0=self.HF, in1=self.MODT[:, sh_off:sh_off + 1024],
                                                   op=ALU.add), ("JUNK", "MODT"), ("JUNK",))
            S.op("act", lambda e: e.copy(out=self.HB[:], in_=self.HF), ("JUNK",), ("HB",))
        else:
            S.op("pool", lambda e: e.tensor_tensor(out=self.HB[:], in0=self.HF, in1=self.MODT[:, sh_off:sh_off + 1024],
                                                   op=ALU.add), ("JUNK", "MODT"), ("HB",))

    def _transpose8(self, src_fn, skey, dst, dkey, pt=0):
        S = self.S
        P = self.PT[pt]
        pk = "PT%d" % pt
        for c in range(8):
            S.op("pe", lambda e, c=c: e.transpose(out=P[:, c * 128:(c + 1) * 128], in_=src_fn(c), identity=self.IDB[:]),
                 (skey, "IDB"), (pk,))
        S.op("act", lambda e: e.copy(out=dst, in_=P[:, 0:1024]), (pk,), (dkey,))

    def _load_w(self, w_ap, l, col0, ncols, dst_col0, key):
        S = self.S
        step = 1024 if ncols % 1024 == 0 else 1536
        for c in range(8):
            for j in range(0, ncols, step):
                kk = "%s_%d_%d" % (key, c, dst_col0 + j)
                if kk not in self.wkeys[c]:
                    self.wkeys[c].append(kk)
                S.dma("pool", lambda e, c=c, j=j: e.dma_start(
                    out=self.WBUF[:, c, dst_col0 + j:dst_col0 + j + step],
                    in_=w_ap[l, c * 128:(c + 1) * 128, col0 + j:col0 + j + step]), (), (kk,))

    def _step_MOD(self, l):
        S = self.S
        S.barrier()
        S.dma("sync", lambda e: e.dma_start(out=self.CB, in_=self.cB), (), ("CB",))
        for half in range(2):
            for c in range(8):
                wa = self.WA[c % 2]
                wk = "WA%d" % (c % 2)
                S.dma("sync", lambda e, wa=wa, c=c, half=half: e.dma_start(
                    out=wa, in_=self.w_ada[l, c * 128:(c + 1) * 128, half * 3072:(half + 1) * 3072]), (), (wk,))
                for n in range(6):
                    S.op("pe", lambda e, wa=wa, c=c, n=n: e.matmul(
                        self.PB[n][:], lhsT=self.CB[:, c, :], rhs=wa[:, n * 512:(n + 1) * 512],
                        start=(c == 0), stop=(c == 7)), (wk, "CB"), ("PB%d" % n,))
            for n in range(6):
                col = half * 3072 + n * 512
                S.dma("sync", lambda e, col=col: e.dma_start(out=self.JUNK[:, 0:512], in_=self.brow[l, :, col:col + 512]),
                      (), ("JUNK",))
                S.op("dve", lambda e, n=n, col=col: e.tensor_tensor(
                    out=self.MODT[:, col:col + 512], in0=self.PB[n][:], in1=self.JUNK[:, 0:512], op=ALU.add),
                     ("PB%d" % n, "JUNK"), ("MODT",))
        for (slot, off) in ((1, 6144), (4, 7168)):
            S.dma("sync", lambda e, off=off: e.dma_start(out=self.JUNK[:, 0:1024], in_=self.brow[l, :, off:off + 1024]),
                  (), ("JUNK",))
            S.op("dve", lambda e, slot=slot: e.scalar_tensor_tensor(
                out=self.MODT[:, slot * 1024:(slot + 1) * 1024], in0=self.MODT[:, slot * 1024:(slot + 1) * 1024],
                scalar=1.0, in1=self.JUNK[:, 0:1024], op0=ALU.add, op1=ALU.mult), ("MODT", "JUNK"), ("MODT",))
        S.barrier()

    def _step_A(self, l, own, xname, pc, kv_only=False):
        S = self.S
        kv = self.kvset(own)
        xsrc = self.xap(xname)
        xkey = "xs_" + xname
        S.barrier()
        self._rope_tables(pc)
        self._load_w(self.w_in, l, 0, 3072, 0, "WBUF")
        S.dma("sync", lambda e: e.dma_start(out=self.QKG[:], in_=self.qkg[l]), (), ("QKG",))
        S.dma("sync", lambda e: e.dma_start(out=self.CONVW[:], in_=self.convw[l]), (), ("CONVW",))
        SM = self.SMALL
        for tt in range(NT):
            xt = self.XT[tt % 2]
            xk = "XT%d" % (tt % 2)
            S.dma("sync", lambda e, xt=xt, tt=tt: e.dma_start(out=xt[:], in_=xsrc[tt * 128:(tt + 1) * 128, :]),
                  (xkey,), (xk,))
            self._rmsnorm(xt, xk, 1024, 0, False)
            self._transpose8(lambda c: self.HB[:, c * 128:(c + 1) * 128], "HB",
                             self.HT[:].rearrange("p a b -> p (a b)"), "HT", 0)
            h0 = 12 if kv_only else 0
            c0, nh = h0 * 64, 24 - h0
            for n in range(6):
                if kv_only and (n == 0 or (n == 5 and tt != NT - 1)):
                    continue
                for c in range(8):
                    S.op("pe", lambda e, n=n, c=c: e.matmul(
                        self.PB[n][:], lhsT=self.HT[:, c, :], rhs=self.WBUF[:, c, n * 512:(n + 1) * 512],
                        start=(c == 0), stop=(c == 7)), ("HT",) + tuple(self.wkeys[c]), ("PB%d" % n,))
                S.op("act", lambda e, n=n: e.copy(out=self.PROJ[:, n * 512:(n + 1) * 512], in_=self.PB[n][:]),
                     ("PB%d" % n,), ("PROJ",))
            S.op("dve", lambda e, c0=c0: e.tensor_tensor(out=self.JUNK[:, c0:1536], in0=self.PROJ[:, c0:1536],
                                                         in1=self.PROJ[:, c0:1536], op=ALU.mult), ("PROJ",), ("JUNK",))
            S.op("dve", lambda e, c0=c0, h0=h0: e.tensor_reduce(
                out=SM[:, 8 + h0:32], in_=self.JUNK[:, c0:1536].rearrange("p (h d) -> p h d", d=64),
                axis=AX.X, op=ALU.add), ("JUNK",), ("SM8",))
            S.op("act", lambda e, h0=h0: e.activation(out=SM[:, 32 + h0:56], in_=SM[:, 8 + h0:32], func=AF.Sqrt, bias=EPS,
                                                      scale=1.0 / 64), ("SM8",), ("SM32",))
            S.op("dve", lambda e, h0=h0: e.reciprocal(out=SM[:, 8 + h0:32], in_=SM[:, 32 + h0:56]), ("SM32",), ("SM8",))
            S.op("dve", lambda e, c0=c0, h0=h0, nh=nh: e.tensor_tensor(
                out=self.JUNK[:, c0:1536].rearrange("p (h d) -> p h d", d=64),
                in0=self.PROJ[:, c0:1536].rearrange("p (h d) -> p h d", d=64),
                in1=SM[:, 8 + h0:32].unsqueeze(2).to_broadcast([128, nh, 64]), op=ALU.mult), ("PROJ", "SM8"), ("JUNK",))
            for s_ in range(1 if kv_only else 0, 2):
                S.op("pool", lambda e, s_=s_: e.tensor_tensor(
                    out=self.JUNK[:, s_ * 768:(s_ + 1) * 768].rearrange("p (h d) -> p h d", d=64),
                    in0=self.JUNK[:, s_ * 768:(s_ + 1) * 768].rearrange("p (h d) -> p h d", d=64),
                    in1=self.QKG[:, s_ * 64:(s_ + 1) * 64].unsqueeze(1).to_broadcast([128, 12, 64]), op=ALU.mult),
                     ("JUNK", "QKG"), ("JUNK",))
            Q3 = self.JUNK[:].rearrange("p (h d) -> p h d", d=64)[:, h0:24, :]
            x1, x2 = Q3[:, :, 0:32], Q3[:, :, 32:64]
            cosb = self.COS[:, tt, :].unsqueeze(1).to_broadcast([128, nh, 32])
            sinb = self.SIN[:, tt, :].unsqueeze(1).to_broadcast([128, nh, 32])
            QR3 = self.QR[:].rearrange("p (h d) -> p h d", d=64)[:, h0:24, :]
            T = [t[:, h0:24, :] for t in self.TR]
            S.op("dve", lambda e, cosb=cosb, x1=x1, x2=x2, T=T: e.tensor_tensor(out=T[0], in0=x1, in1=cosb, op=ALU.mult), ("JUNK", "COS"), ("T0",))
            S.op("pool", lambda e, sinb=sinb, x1=x1, x2=x2, T=T: e.tensor_tensor(out=T[1], in0=x2, in1=sinb, op=ALU.mult), ("JUNK", "SIN"), ("T1",))
            S.op("dve", lambda e, QR3=QR3, T=T: e.tensor_tensor(out=QR3[:, :, 0:32], in0=T[0], in1=T[1], op=ALU.subtract),
                 ("T0", "T1"), ("QR",))
            S.op("dve", lambda e, cosb=cosb, x1=x1, x2=x2, T=T: e.tensor_tensor(out=T[0], in0=x2, in1=cosb, op=ALU.mult), ("JUNK", "COS"), ("T0",))
            S.op("pool", lambda e, sinb=sinb, x1=x1, x2=x2, T=T: e.tensor_tensor(out=T[1], in0=x1, in1=sinb, op=ALU.mult), ("JUNK", "SIN"), ("T1",))
            S.op("pool", lambda e, QR3=QR3, T=T: e.tensor_tensor(out=QR3[:, :, 32:64], in0=T[0], in1=T[1], op=ALU.add),
                 ("T0", "T1"), ("QR",))
            for j in range(6 if kv_only else 0, 12):
                P = self.PT[0] if j < 6 else self.PT[1]
                pk = "PT0" if j < 6 else "PT1"
                jj = j % 6
                S.op("pe", lambda e, j=j, jj=jj, P=P: e.transpose(
                    out=P[:, jj * 128:(jj + 1) * 128], in_=self.QR[:, j * 128:(j + 1) * 128], identity=self.IDB[:]),
                     ("QR", "IDB"), (pk,))
            if not kv_only:
                S.op("act", lambda e: e.copy(out=self.QTS[:], in_=self.PT[0][:, 0:768]), ("PT0",), ("QTS",))
            S.op("act", lambda e: e.copy(out=self.KTS[:], in_=self.PT[1][:, 0:768]), ("PT1",), ("KTS",))
            qtd = self.qtd[:, :, tt * 128:(tt + 1) * 128].rearrange("h p t -> p h t")
            ktd = kv["kt"][:, :, tt * 128:(tt + 1) * 128].rearrange("h p t -> p h t")
            if not kv_only:
                S.dma("sync", lambda e, qtd=qtd: e.dma_start(out=qtd, in_=self.QTS[:].rearrange("p (h t) -> p h t", t=128)),
                      ("QTS",), ("qtd",))
            S.dma("sync", lambda e, ktd=ktd: e.dma_start(out=ktd, in_=self.KTS[:].rearrange("p (h t) -> p h t", t=128)),
                  ("KTS",), ("ktd_" + own,))
            S.op("act", lambda e: e.copy(out=self.VES[:, :, 0:64],
                                         in_=self.PROJ[:, 1536:2304].rearrange("p (h d) -> p h d", d=64)),
                 ("PROJ",), ("VES",))
            ved = kv["ve"][:, tt, :, :].rearrange("h p f -> p h f")
            S.dma("sync", lambda e, ved=ved: e.dma_start(out=ved, in_=self.VES[:].rearrange("p (h e) f -> p h (e f)", e=2)),
                  ("VES",), ("ved_" + own,))
            if kv_only:
                if tt == NT - 1:
                    S.op("dve", lambda e: e.tensor_tensor(out=self.UC[:], in0=self.PROJ[:, 2560:2816],
                                                          in1=self.PROJ[:, 2816:3072], op=ALU.mult), ("PROJ",), ("UC",))
                    S.dma("sync", lambda e: e.dma_start(out=kv["u"], in_=self.UC[126:128, :]), ("UC",), ("u_" + own,))
                continue
            uw = self.UW[tt % 2] if tt > 0 else self.UW0
            uwk = ("UW%d" % (tt % 2)) if tt > 0 else "UW0"
            S.op("dve", lambda e: e.tensor_tensor(out=self.UC[:], in0=self.PROJ[:, 2560:2816], in1=self.PROJ[:, 2816:3072],
                                                  op=ALU.mult), ("PROJ",), ("UC",))
            S.op("pool", lambda e, uw=uw: e.tensor_tensor(
                out=uw[:].rearrange("p (k c) -> p k c", k=3),
                in0=self.UC[:].unsqueeze(1).to_broadcast([128, 3, 256]),
                in1=self.CONVW[:].rearrange("p (k c) -> p k c", k=3), op=ALU.mult), ("UC", "CONVW"), (uwk,))
            if tt == NT - 1:
                S.dma("sync", lambda e: e.dma_start(out=kv["u"], in_=self.UC[126:128, :]), ("UC",), ("u_" + own,))
            if tt == 0:
                S.op("act", lambda e: e.copy(out=self.B0[:], in_=self.PROJ[:, 2304:2560]), ("PROJ",), ("B0",))
            else:
                prev = self.UW[(tt - 1) % 2] if tt > 1 else self.UW0
                pk_ = ("UW%d" % ((tt - 1) % 2)) if tt > 1 else "UW0"
                self._conv_mm(uw, uwk, prev, pk_)
                S.op("dve", lambda e, tt=tt: e.tensor_tensor(out=self.MIX[:, tt, 768:1024], in0=self.PB[0][:, 0:256],
                                                             in1=self.PROJ[:, 2304:2560], op=ALU.mult),
                     ("PB0", "PROJ"), ("MIX",))

    def _conv_mm(self, uw, uwk, prev, pk_):
        S = self.S
        Y = self.PB[0][:, 0:256]
        terms = [(0, uw, uwk, 0), (1, uw, uwk, 1), (None, uw, uwk, 2), (2, prev, pk_, 0), (3, prev, pk_, 1)]
        for i, (m, src, sk, k) in enumerate(terms):
            lhs = self.IDF[:] if m is None else self.SHM[:, m, :]
            S.op("pe", lambda e, lhs=lhs, src=src, k=k, i=i: e.matmul(
                Y, lhsT=lhs, rhs=src[:, k * 256:(k + 1) * 256], start=(i == 0), stop=(i == 4)),
                 (sk, "SHM", "IDF"), ("PB0",))

    def _step_B(self, l, own, halo, xname, fc, cvt=None, peer=True):
        S = self.S
        xs_ = self.xs[xname]
        xkey = "xs_" + xname
        kvo = self.kvset(own)
        kvh = self.kvset(halo)
        S.barrier()
        if cvt is not None:
            self._step_CVT(cvt)
        S.dma("sync", lambda e: e.dma_start(out=self.MASK[:], in_=self.c_mask), (), ("MASK",))
        S.op("pool", lambda e: e.memset(self.UH[:], 0.0), (), ("UH",))
        S.dma("sync", lambda e: e.dma_start(out=self.UH[126:128, :], in_=kvh["u"]), ("u_" + halo,), ("UH",))
        S.op("dve", lambda e: e.tensor_scalar(out=self.UH[:], in0=self.UH[:], scalar1=self.FLAG[:, fc:fc + 1], scalar2=None,
                                              op0=ALU.mult), ("UH", "FLAG"), ("UH",))
        S.op("pool", lambda e: e.tensor_tensor(
            out=self.UWH[:].rearrange("p (k c) -> p k c", k=3),
            in0=self.UH[:].unsqueeze(1).to_broadcast([128, 3, 256]),
            in1=self.CONVW[:].rearrange("p (k c) -> p k c", k=3), op=ALU.mult), ("UH", "CONVW"), ("UWH",))
        self._conv_mm(self.UW0, "UW0", self.UWH, "UWH")
        S.op("dve", lambda e: e.tensor_tensor(out=self.MIX[:, 0, 768:1024], in0=self.PB[0][:, 0:256], in1=self.B0[:],
                                              op=ALU.mult), ("PB0", "B0"), ("MIX",))
        for hp in range(6):
            S.dma("sync", lambda e, hp=hp: e.dma_start(out=self.KT[:, 0:TPC], in_=kvh["kt"][hp]),
                  ("ktd_" + halo,), ("KTh",))
            S.dma("sync", lambda e, hp=hp: e.dma_start(out=self.KT[:, TPC:2 * TPC], in_=kvo["kt"][hp]),
                  ("ktd_" + own,), ("KTo",))
            S.dma("sync", lambda e, hp=hp: e.dma_start(out=self.QT[:], in_=self.qtd[hp]), ("qtd",), ("QT",))
            S.dma("sync", lambda e, hp=hp: e.dma_start(out=self.VE[:, 0:NT, :],
                                                       in_=kvh["ve"][hp].rearrange("b p f -> p b f")),
                  ("ved_" + halo,), ("VEh",))
            S.dma("sync", lambda e, hp=hp: e.dma_start(out=self.VE[:, NT:2 * NT, :],
                                                       in_=kvo["ve"][hp].rearrange("b p f -> p b f")),
                  ("ved_" + own,), ("VEo",))
            S.op("pool", lambda e: e.tensor_scalar(out=self.VE[:, 0:NT, :], in0=self.VE[:, 0:NT, :],
                                                   scalar1=self.FLAG[:, fc:fc + 1], scalar2=None, op0=ALU.mult),
                 ("VEh", "FLAG"), ("VEh",))
            its = []
            for e2 in range(2):
                for g in range(4):
                    kbs = [kb for kb in range(g * 4, g * 4 + 20)
                           if any(0 <= (16 + g * 4) - kb + m <= 16 for m in range(4))]
                    for kb in kbs:
                        its.append((e2, g, kb, kb == kbs[-1]))
            LA = 2

            def front(it, e2, g, kb):
                p0, p1 = e2 * 64, (e2 + 1) * 64
                d0 = (16 + g * 4) - kb
                SB_, sk = self.SBK[it % 4]
                pe_, pek = self.PE_[it % 4], "PE%d" % (it % 4)
                pm, pmk = self.PM[it % 4], "PM%d" % (it % 4)
                S.op("pe", lambda e: e.matmul(
                    SB_, lhsT=self.KT[p0:p1, kb * 128:(kb + 1) * 128],
                    rhs=self.QT[p0:p1, g * 512:(g + 1) * 512], start=True, stop=True),
                     ("KTh" if kb < NT else "KTo", "QT"), (sk,))
                S.op("act", lambda e: e.activation(out=pe_, in_=SB_, func=AF.Exp, scale=0.125), (sk,), (pek,))
                mc = (d0 + 3) * 128
                S.op("dve", lambda e: e.tensor_tensor(out=pm, in0=pe_, in1=self.MASK[:, mc:mc + 512], op=ALU.mult),
                     (pek, "MASK"), (pmk,))

            def back(it, e2, g, kb, last):
                head = hp * 2 + e2
                d0 = (16 + g * 4) - kb
                pm, pmk = self.PM[it % 4], "PM%d" % (it % 4)
                for m in range(4):
                    dl = d0 + m
                    if not (0 <= dl <= 16):
                        continue
                    vkey = "VEh" if kb < NT else "VEo"
                    S.op("pe", lambda e, m=m, dl=dl: e.matmul(
                        self.PB[2 + m][:, 0:65], lhsT=pm[:, m * 128:(m + 1) * 128],
                        rhs=self.VE[:, kb, e2 * 65:(e2 + 1) * 65], start=(dl == 16), stop=(dl == 0)),
                         (pmk, vkey), ("PB%d" % (2 + m),))
                if last:
                    for m in range(4):
                        ok = "PB%d" % (2 + m)
                        S.op("dve", lambda e, m=m: e.reciprocal(out=self.RC[:, m:m + 1], in_=self.PB[2 + m][:, 64:65]),
                             (ok,), ("RC%d" % m,))
                        S.op("dve", lambda e, m=m: e.tensor_scalar(
                            out=self.MIX[:, g * 4 + m, head * 64:(head + 1) * 64], in0=self.PB[2 + m][:, 0:64],
                            scalar1=self.RC[:, m:m + 1], scalar2=None, op0=ALU.mult), (ok, "RC%d" % m), ("MIX",))

            for idx in range(len(its) + LA):
                if idx < len(its):
                    front(idx, *its[idx][:3])
                if idx - LA >= 0:
                    back(idx - LA, *its[idx - LA])
        S.barrier()
        self._load_w(self.w_out, l, 0, 1024, 0, "WBUF")
        self._load_w(self.w_q, l, 0, 2048, 1024, "WBUF")
        for j in range(2):
            S.dma("pool", lambda e, j=j: e.dma_start(out=self.KEYS[:, j * 1024:(j + 1) * 1024],
                                                     in_=self.keysT[l, :, j * 1024:(j + 1) * 1024]), (), ("KEYS%d" % j,))
        for tt in range(NT):
            xt = self.XT[tt % 2]
            xk = "XT%d" % (tt % 2)
            S.dma("sync", lambda e, xt=xt, tt=tt: e.dma_start(out=xt[:], in_=xs_[tt * 128:(tt + 1) * 128, :]),
                  (xkey,), (xk,))
            self._transpose8(lambda c, tt=tt: self.MIX[:, tt, c * 128:(c + 1) * 128], "MIX",
                             self.HT[:].rearrange("p a b -> p (a b)"), "HT", 0)
            for n in range(2):
                for c in range(8):
                    S.op("pe", lambda e, n=n, c=c: e.matmul(
                        self.PB[4 + n][:], lhsT=self.HT[:, c, :], rhs=self.WBUF[:, c, n * 512:(n + 1) * 512],
                        start=(c == 0), stop=(c == 7)), ("HT",) + tuple(self.wkeys[c]), ("PB%d" % (4 + n),))
                S.op("dve", lambda e, n=n: e.tensor_tensor(
                    out=self.HF[:, n * 512:(n + 1) * 512], in0=self.PB[4 + n][:],
                    in1=self.MODT[:, 2048 + n * 512:2048 + (n + 1) * 512], op=ALU.mult),
                     ("PB%d" % (4 + n), "MODT"), ("JUNK",))
            S.op("pool", lambda e, xt=xt: e.tensor_tensor(out=xt[:], in0=xt[:], in1=self.HF, op=ALU.add),
                 (xk, "JUNK"), (xk,))
            S.dma("sync", lambda e, xt=xt, tt=tt: e.dma_start(out=xs_[tt * 128:(tt + 1) * 128, :], in_=xt[:]),
                  (xk,), (xkey,))
        S.barrier()
        if peer:
            self._peer(l, xname)

    def _peer(self, l, xname):
        S = self.S
        xs_ = self.xs[xname]
        xkey = "xs_" + xname
        SM = self.SMALL
        def prologue(tt):
            par = tt % 2
            GATE, gk = self.GATEB[par], "GATE%d" % par
            EIDX, ek = self.EIDXB[par], "EIDX%d" % par
            HBD, hk = self.HBD[par], "HBD%d" % par
            xt = self.XT[tt % 2]
            xk = "XT%d" % (tt % 2)
            S.dma("sync", lambda e, xt=xt, tt=tt: e.dma_start(out=xt[:], in_=xs_[tt * 128:(tt + 1) * 128, :]),
                  (xkey,), (xk,))
            self._rmsnorm(xt, xk, 4096, 3072, True)
            S.op("act", lambda e: e.copy(out=HBD[:], in_=self.HF), ("JUNK",), (hk,))
            self._transpose8(lambda c: self.HB[:, c * 128:(c + 1) * 128], "HB",
                             self.HT[:].rearrange("p a b -> p (a b)"), "HT", 0)
            for n in range(4):
                for c in range(8):
                    S.op("pe", lambda e, n=n, c=c: e.matmul(
                        self.PB[n][:], lhsT=self.HT[:, c, :], rhs=self.WBUF[:, c, 1024 + n * 512:1024 + (n + 1) * 512],
                        start=(c == 0), stop=(c == 7)), ("HT",) + tuple(self.wkeys[c]), ("PB%d" % n,))
                S.op("act", lambda e, n=n: e.copy(out=self.QP[:, n * 512:(n + 1) * 512], in_=self.PB[n][:]),
                     ("PB%d" % n,), ("QP",))
            for half in range(2):
                self._transpose8(lambda c, half=half: self.QP[:, (half * 8 + c) * 128:(half * 8 + c + 1) * 128], "QP",
                                 self.QPT[:, half * 8:(half + 1) * 8, :].rearrange("p a b -> p (a b)"), "QPT", half)
            for j in range(16):
                S.op("pe", lambda e, j=j: e.matmul(
                    self.PB[j // 4][:, (j % 4) * 128:(j % 4 + 1) * 128], lhsT=self.QPT[:, j, :],
                    rhs=self.KEYS[:, j * 128:(j + 1) * 128], start=True, stop=True), ("QPT", "KEYS0", "KEYS1"), ("PB%d" % (j // 4),))
            for n in range(4):
                S.op("act", lambda e, n=n: e.copy(out=self.SC[:, n * 4:(n + 1) * 4, :].rearrange("p a b -> p (a b)"),
                                                  in_=self.PB[n][:]), ("PB%d" % n,), ("SC",))
            for j in range(16):
                S.op("dve", lambda e, j=j: e.max(out=self.V16[:, j, 0:8], in_=self.SC[:, j, :]), ("SC",), ("V16",))
                S.op("dve", lambda e, j=j: e.match_replace(out=self.SCM[:, 0:128], in_to_replace=self.V16[:, j, 0:8],
                                                           in_values=self.SC[:, j, :], imm_value=-1e30),
                     ("SC", "V16"), ("SCM",))
                S.op("dve", lambda e, j=j: e.max(out=self.V16[:, j, 8:16], in_=self.SCM[:, 0:128]), ("SCM",), ("V16",))
                S.op("dve", lambda e, j=j: e.max_index(out=self.IX[:, j, 0:8], in_max=self.V16[:, j, 0:8],
                                                       in_values=self.SC[:, j, :]), ("SC", "V16"), ("IX",))
                S.op("dve", lambda e, j=j: e.max_index(out=self.IX[:, j, 8:16], in_max=self.V16[:, j, 8:16],
                                                       in_values=self.SC[:, j, :]), ("SC", "V16"), ("IX",))
            S.op("dve", lambda e: e.tensor_copy(out=self.IXF, in_=self.IX), ("IX",), ("IXF",))
            V4 = self.V16.rearrange("p (h s) r -> p h s r", s=2)
            I4 = self.IXF.rearrange("p (h s) r -> p h s r", s=2)
            C4 = self.CAND.rearrange("p h (a b) -> p h a b", b=16)
            S.op("dve", lambda e: e.tensor_tensor(
                out=C4, in0=V4[:, :, 0, :].unsqueeze(3).to_broadcast([128, 8, 16, 16]),
                in1=V4[:, :, 1, :].unsqueeze(2).to_broadcast([128, 8, 16, 16]), op=ALU.add), ("V16",), ("CAND",))
            for h in range(8):
                S.op("dve", lambda e, h=h: e.max(out=self.BEST[:, h, 0:8], in_=self.CAND[:, h, :]), ("CAND",), ("BEST",))
                S.op("dve", lambda e, h=h: e.match_replace(out=self.SCM, in_to_replace=self.BEST[:, h, 0:8],
                                                           in_values=self.CAND[:, h, :], imm_value=-1e30),
                     ("CAND", "BEST"), ("SCM",))
                S.op("dve", lambda e, h=h: e.max(out=self.BEST[:, h, 8:16], in_=self.SCM), ("SCM",), ("BEST",))
                S.op("dve", lambda e, h=h: e.max_index(out=self.POS[:, h, 0:8], in_max=self.BEST[:, h, 0:8],
                                                       in_values=self.CAND[:, h, :]), ("CAND", "BEST"), ("POS",))
                S.op("dve", lambda e, h=h: e.max_index(out=self.POS[:, h, 8:16], in_max=self.BEST[:, h, 8:16],
                                                       in_values=self.CAND[:, h, :]), ("CAND", "BEST"), ("POS",))
            S.op("dve", lambda e: e.tensor_tensor(out=GATE, in0=self.BEST,
                                                  in1=self.BEST[:, :, 0:1].to_broadcast([128, 8, 16]), op=ALU.subtract),
                 ("BEST",), (gk,))
            S.op("act", lambda e: e.activation(out=GATE, in_=GATE, func=AF.Exp), (gk,), (gk,))
            S.op("dve", lambda e: e.tensor_reduce(out=SM[:, 56:64], in_=GATE, axis=AX.X, op=ALU.add),
                 (gk,), ("SM56",))
            S.op("dve", lambda e: e.reciprocal(out=SM[:, 56:64], in_=SM[:, 56:64]), ("SM56",), ("SM56",))
            S.op("dve", lambda e: e.tensor_tensor(out=GATE, in0=GATE,
                                                  in1=SM[:, 56:64].unsqueeze(2).to_broadcast([128, 8, 16]), op=ALU.mult),
                 (gk, "SM56"), (gk,))
            S.op("dve", lambda e: e.tensor_single_scalar(out=self.R0, in_=self.POS, scalar=4,
                                                         op=ALU.logical_shift_right), ("POS",), ("R0",))
            S.op("dve", lambda e: e.tensor_single_scalar(out=self.R1, in_=self.POS, scalar=15,
                                                         op=ALU.bitwise_and), ("POS",), ("R1",))
            S.op("dve", lambda e: e.tensor_copy(out=self.R0F, in_=self.R0), ("R0",), ("R0F",))
            S.op("dve", lambda e: e.tensor_copy(out=self.R1F, in_=self.R1), ("R1",), ("R1F",))
            iob = self.IOTA16[:].unsqueeze(1).unsqueeze(1).to_broadcast([128, 8, 16, 16])
            for (rf, rk, side, dst, dk) in ((self.R0F, "R0F", 0, self.ISEL, "ISEL"), (self.R1F, "R1F", 1, self.JSEL, "JSEL")):
                S.op("dve", lambda e, rf=rf: e.tensor_tensor(
                    out=self.OH, in0=iob, in1=rf.unsqueeze(3).to_broadcast([128, 8, 16, 16]), op=ALU.is_equal),
                     (rk, "IOTA16"), ("OH",))
                S.op("dve", lambda e, side=side: e.tensor_tensor(
                    out=self.OH, in0=self.OH, in1=I4[:, :, side, :].unsqueeze(2).to_broadcast([128, 8, 16, 16]),
                    op=ALU.mult), ("OH", "IXF"), ("OH",))
                S.op("dve", lambda e, dst=dst: e.tensor_reduce(out=dst, in_=self.OH, axis=AX.X, op=ALU.add),
                     ("OH",), (dk,))
            S.op("dve", lambda e: e.scalar_tensor_tensor(out=self.ISEL, in0=self.ISEL, scalar=128.0, in1=self.JSEL,
                                                         op0=ALU.mult, op1=ALU.add), ("ISEL", "JSEL"), ("ISEL",))
            if l > 0:
                S.op("dve", lambda e: e.tensor_scalar(out=self.ISEL, in0=self.ISEL, scalar1=float(l * NEXP), scalar2=None,
                                                      op0=ALU.add), ("ISEL",), ("ISEL",))
            S.op("dve", lambda e: e.tensor_copy(out=EIDX, in_=self.ISEL.rearrange("p h r -> p (h r)")),
                 ("ISEL",), (ek,))

        def loop(tt, pending):
            xt = self.XT[tt % 2]
            xk = "XT%d" % (tt % 2)
            par = tt % 2
            GATE, gk = self.GATEB[par], "GATE%d" % par
            EIDX, ek = self.EIDXB[par], "EIDX%d" % par
            HBD, hk = self.HBD[par], "HBD%d" % par
            uv2d = self.uvb.rearrange("l e d -> (l e) d")
            gflat = GATE.rearrange("p h r -> p (h r)")
            GS = 2
            nrec = len(pending)
            per_group = -(-nrec // (124 // GS)) if nrec else 0
            for gb in range(0, 128, GS):
                ak, zk = "ACTV%d_%d" % (tt, gb), "ZZ%d_%d" % (tt, gb)
                for k in range(gb, gb + GS):
                    uv, uvk = self.UVG[k % 8], "UVG%d" % (k % 8)
                    S.dma("pool", lambda e, uv=uv, k=k: e.indirect_dma_start(
                        out=uv, out_offset=None, in_=uv2d,
                        in_offset=bass.IndirectOffsetOnAxis(ap=EIDX[:, k:k + 1], axis=0)), (ek,), (uvk,))
                    S.op("dve", lambda e, uv=uv, k=k: e.scalar_tensor_tensor(
                        out=self.QR[:, 0:1024], in0=uv[:, 0:1024], scalar=1.0, in1=HBD[:], op0=ALU.mult, op1=ALU.mult,
                        accum_out=self.ACTV[:, k:k + 1]), (uvk, hk), (ak,))
                S.op("act", lambda e, gb=gb: e.activation(out=self.ZZ[:, gb:gb + GS], in_=self.ACTV[:, gb:gb + GS],
                                                         func=AF.Gelu), (ak,), (zk,))
                for k in range(gb, gb + GS):
                    S.op("act", lambda e, k=k: e.activation(out=self.ZZ[:, k:k + 1], in_=self.ZZ[:, k:k + 1], func=AF.Copy,
                                                            scale=gflat[:, k:k + 1]), (zk, gk), (zk,))
                for k in range(gb, gb + GS):
                    uv, uvk = self.UVG[k % 8], "UVG%d" % (k % 8)
                    dk, dkk = self.DK[k % 4], "DK%d" % (k % 4)
                    S.op("act", lambda e, dk=dk, k=k: e.activation(out=dk, in_=self.IDF[:], func=AF.Copy,
                                                                    scale=self.ZZ[:, k:k + 1]), ("IDF", zk), (dkk,))
                    for n in range(2):
                        S.op("pe", lambda e, dk=dk, uv=uv, n=n, k=k: e.matmul(
                            self.PB[4 + n][:], lhsT=dk, rhs=uv[:, 1024 + n * 512:1024 + (n + 1) * 512],
                            start=(k == 0), stop=(k == 127)), (dkk, uvk), ("PB%d" % (4 + n),))
                if pending:
                    S.replay_records(pending[:per_group])
                    del pending[:per_group]
            if pending:
                S.replay_records(pending)
                del pending[:]

        def epilogue(tt):
            xt = self.XT[tt % 2]
            xk = "XT%d" % (tt % 2)
            for n in range(2):
                S.op("dve", lambda e, n=n: e.tensor_tensor(
                    out=self.HF[:, n * 512:(n + 1) * 512], in0=self.PB[4 + n][:],
                    in1=self.MODT[:, 5120 + n * 512:5120 + (n + 1) * 512], op=ALU.mult),
                     ("PB%d" % (4 + n), "MODT"), ("JUNK",))
            S.op("pool", lambda e, xt=xt: e.tensor_tensor(out=xt[:], in0=xt[:], in1=self.HF, op=ALU.add),
                 (xk, "JUNK"), (xk,))
            S.dma("sync", lambda e, xt=xt, tt=tt: e.dma_start(out=xs_[tt * 128:(tt + 1) * 128, :], in_=xt[:]),
                  (xk,), (xkey,))


        S.wait_cvt("pool")
        prologue(0)
        for tt in range(NT):
            pending = []
            if tt + 1 < NT:
                S.rec = []
                prologue(tt + 1)
                pending = S.rec
                S.rec = None
            loop(tt, pending)
            epilogue(tt)

    def _finish(self):
        S = self.S
        S.barrier()
        if self.x_out is not None:
            S.dma("sync", lambda e: e.dma_start(out=self.x_out, in_=self.xs["c0"]), ("xs_c0",), ("x_out",))
        S.barrier()


def _build(plan, ext):
    b = Builder(plan, ext)
    nc = b.build()
    return nc, list(b.used_inputs)


def _common_inputs(x, c, positions, w_ada, b_ada, norm_mix, norm_ffn, w_in, q_norm, k_norm, conv_w, w_out,
                   peer_wq, peer_keys, peer_u, peer_v):
    f = np.float32
    cons = _consts()
    cvec = np.asarray(c, f).reshape(8, 128)
    cB = np.ascontiguousarray(np.broadcast_to(cvec.T[:, :, None], (128, 8, 128))).astype(f)
    brow = np.concatenate([np.asarray(b_ada, f), np.asarray(norm_mix, f), np.asarray(norm_ffn, f)], axis=1)
    brow = np.ascontiguousarray(np.broadcast_to(brow[:, None, :], (2, 128, 8192)))
    qk = np.concatenate([np.asarray(q_norm, f), np.asarray(k_norm, f)], axis=1)
    qkg = np.ascontiguousarray(np.broadcast_to(qk[:, None, :], (2, 128, 128)))
    cw = np.asarray(conv_w, f).reshape(2, 768)
    convw = np.ascontiguousarray(np.broadcast_to(cw[:, None, :], (2, 128, 768)))
    keysT = np.ascontiguousarray(np.asarray(peer_keys, f).transpose(0, 4, 1, 2, 3).reshape(2, 128, 2048))
    shared = dict(cB=cB, w_ada=np.asarray(w_ada, f), brow=brow, w_in=np.asarray(w_in, f), w_out=np.asarray(w_out, f),
                  w_q=np.asarray(peer_wq, f), qkg=qkg, convw=convw, keysT=keysT, peer_u=np.asarray(peer_u, f),
                  peer_v=np.asarray(peer_v, f), identb=cons["identb"], identf=cons["identf"], mask=cons["mask"],
                  freq=cons["freq"], shm=cons["shm"], iota16=cons["iota16"])
    xs = np.asarray(x, f).reshape(NCORES, TPC, D)
    pos = np.asarray(positions).astype(np.int32).reshape(NCORES, NT, 128)
    zx = np.zeros((TPC, D), f)
    zp = np.zeros((NT, 128), np.int32)
    per = []
    for i in range(NCORES):
        d = dict(shared)
        d["x0"] = np.ascontiguousarray(xs[i])
        d["xm1"] = np.ascontiguousarray(xs[i - 1]) if i >= 1 else zx
        d["xm2"] = np.ascontiguousarray(xs[i - 2]) if i >= 2 else zx
        pp = [pos[i - 2] if i >= 2 else zp, pos[i - 1] if i >= 1 else zp, pos[i]]
        d["posi"] = np.ascontiguousarray(np.concatenate([p.T for p in pp], axis=1))
        d["flag"] = np.array([[1.0 if i >= 1 else 0.0, 1.0 if i >= 2 else 0.0]] * 128, f)
        per.append(d)
    return per


_PROGS = {}


def _prog(name, plan, ext):
    if name not in _PROGS:
        _PROGS[name] = _build(plan, ext)
    return _PROGS[name]


def _run(prog, per, extra=None):
    nc, used = prog
    maps = []
    for i in range(NCORES):
        d = {k: per[i][k] for k in used if k in per[i]}
        if extra is not None:
            d.update(extra[i])
        maps.append(d)
    return run_bass_kernel_spmd(nc, maps, core_ids=list(range(NCORES))).results


PLAN = [("MOD", 0),
        ("A", 0, "kvA", "xm2", 0, True),
        ("A", 0, "kvB", "m1", 1), ("B", 0, "kvB", "kvA", "m1", 1, 0),
        ("A", 0, "kvC", "c0", 2), ("B", 0, "kvC", "kvB", "c0", 0, 1),
        ("MOD", 1),
        ("A", 1, "kvD", "m1", 1, True),
        ("A", 1, "kvE", "c0", 2), ("B", 1, "kvE", "kvD", "c0", 0)]


def kernel(**inputs):
    per = _common_inputs(**inputs)
    prog = _prog("fused", PLAN, {"x_out": "out"})
    r = _run(prog, per)
    out = np.stack([r[i]["x_out"] for i in range(NCORES)], axis=0).reshape(1, SEQ, D)
    return out.astype(np.float32)
```
